# Optimizing a Trainium2 kernel written in Bass

```python
import math
import jax, jax.numpy as jnp
from jax import lax
import numpy as np

D_MODEL = 2048
BATCH = 1
SEQ = 16384
DEPTH = 2

SGU_CHUNK = 128
SGU_GROUPS = 8
SGU_HEAD = 128
SGU_WIDTH = SGU_GROUPS * SGU_HEAD
MLA_HEADS = 8
MLA_Q_RANK = 512
MLA_KV_RANK = 512
MLA_NOPE = 128
MLA_ROPE = 64
MLA_V = 128
ROPE_THETA = 10000.0
ATTN_BLOCK = 128
HYB_IN = 2 * SGU_WIDTH + MLA_Q_RANK + MLA_KV_RANK + MLA_ROPE
HYB_MIX = SGU_WIDTH + MLA_HEADS * MLA_V
SSM_INNER = 2 * D_MODEL
SSM_HEAD_DIM = 64
SSM_HEADS = SSM_INNER // SSM_HEAD_DIM
SSM_GROUPS = 8
SSM_STATE = 128
SSM_CONV = 5
SSM_CHUNK = 256
SSM_CONV_CH = SSM_INNER + 2 * SSM_GROUPS * SSM_STATE
SSM_IN = SSM_INNER + SSM_CONV_CH + 2 * SSM_HEADS
FFN_HIDDEN = 4 * D_MODEL
N_EVEN = (DEPTH + 1) // 2
N_ODD = DEPTH // 2
EPS = 1e-6

kernel_name = 'hybrid_gmlp_mla_mamba2_adaln_encoder'


def rmsnorm(x, g):
    xf = x.astype(jnp.float32)
    y = xf * lax.rsqrt(jnp.mean(xf * xf, axis=-1, keepdims=True) + EPS)
    return (y * g.astype(jnp.float32)).astype(x.dtype)


def layernorm(x, g):
    xf = x.astype(jnp.float32)
    xc = xf - jnp.mean(xf, axis=-1, keepdims=True)
    y = xc * lax.rsqrt(jnp.mean(xc * xc, axis=-1, keepdims=True) + EPS)
    return (y * g.astype(jnp.float32)).astype(x.dtype)


def rope(x, cos, sin):
    half = x.shape[-1] // 2
    xf = x.astype(jnp.float32)
    x1, x2 = xf[..., :half], xf[..., half:]
    return jnp.concatenate([x1 * cos - x2 * sin, x2 * cos + x1 * sin], axis=-1).astype(x.dtype)


def gmlp_sgu(u, v, norm_g, w_s, b_s):
    b, s, _ = v.shape
    v = layernorm(v, norm_g)
    vc = v.reshape(b, s // SGU_CHUNK, SGU_CHUNK, SGU_GROUPS, SGU_HEAD)
    mixed = jnp.einsum('gij,bcjgd->bcigd', w_s, vc) + b_s.T[None, None, :, :, None]
    return u * mixed.reshape(b, s, SGU_WIDTH)


def mla_attention(q_lat, kv_lat, k_pe, positions, q_norm_g, kv_norm_g, w_uq, w_ukv):
    b, s, _ = q_lat.shape
    q = (rmsnorm(q_lat, q_norm_g) @ w_uq).reshape(b, s, MLA_HEADS, MLA_NOPE + MLA_ROPE)
    kv = (rmsnorm(kv_lat, kv_norm_g) @ w_ukv).reshape(b, s, MLA_HEADS, MLA_NOPE + MLA_V)
    q_nope, q_pe = q[..., :MLA_NOPE], q[..., MLA_NOPE:]
    k_nope, v = kv[..., :MLA_NOPE], kv[..., MLA_NOPE:]
    half = MLA_ROPE // 2
    inv_freq = ROPE_THETA ** (-jnp.arange(half, dtype=jnp.float32) / half)
    ang = positions.astype(jnp.float32)[..., None] * inv_freq
    cos, sin = jnp.cos(ang), jnp.sin(ang)
    q_pe = rope(q_pe, cos[:, :, None], sin[:, :, None])
    k_pe = rope(k_pe, cos, sin)
    scale = (MLA_NOPE + MLA_ROPE) ** -0.5
    nb = s // ATTN_BLOCK
    qn_b = q_nope.reshape(b, nb, ATTN_BLOCK, MLA_HEADS, MLA_NOPE).transpose(1, 0, 2, 3, 4)
    qp_b = q_pe.reshape(b, nb, ATTN_BLOCK, MLA_HEADS, MLA_ROPE).transpose(1, 0, 2, 3, 4)

    def attend(blk):
        qn, qp = blk
        sc = (jnp.einsum('bqhd,bkhd->bhqk', qn, k_nope, preferred_element_type=jnp.float32)
              + jnp.einsum('bqhr,bkr->bhqk', qp, k_pe, preferred_element_type=jnp.float32))
        p = jax.nn.softmax(sc * scale, axis=-1).astype(v.dtype)
        return jnp.einsum('bhqk,bkhd->bqhd', p, v)

    o = lax.map(attend, (qn_b, qp_b))
    return o.transpose(1, 0, 2, 3, 4).reshape(b, s, MLA_HEADS * MLA_V)


def hybrid_mixer(h, positions, w_in, sgu_norm_g, sgu_w, sgu_b, q_norm_g, kv_norm_g, w_uq, w_ukv, w_out):
    proj = h @ w_in
    cuts = [SGU_WIDTH, 2 * SGU_WIDTH, 2 * SGU_WIDTH + MLA_Q_RANK,
            2 * SGU_WIDTH + MLA_Q_RANK + MLA_KV_RANK]
    u, v, q_lat, kv_lat, k_pe = jnp.split(proj, cuts, axis=-1)
    a_out = gmlp_sgu(jax.nn.gelu(u), jax.nn.gelu(v), sgu_norm_g, sgu_w, sgu_b)
    b_out = mla_attention(q_lat, kv_lat, k_pe, positions, q_norm_g, kv_norm_g, w_uq, w_ukv)
    return jnp.concatenate([a_out, b_out], axis=-1) @ w_out


def depthwise_conv_centred(x, w, bias):
    ch = x.shape[-1]
    y = lax.conv_general_dilated(x, w[:, None, :].astype(x.dtype), window_strides=(1,),
                                 padding=[(SSM_CONV // 2, SSM_CONV // 2)],
                                 dimension_numbers=('NWC', 'WIO', 'NWC'),
                                 feature_group_count=ch)
    return y + bias


def ssd_scan(x, dt, a, bm, cm):
    b, s = x.shape[:2]
    pad = (-s) % SSM_CHUNK
    if pad:
        padw = lambda t: jnp.pad(t, [(0, 0), (0, pad)] + [(0, 0)] * (t.ndim - 2))
        x, dt, bm, cm = padw(x), padw(dt), padw(bm), padw(cm)
    sp = s + pad
    nc = sp // SSM_CHUNK
    hpg = SSM_HEADS // SSM_GROUPS

    def chunks(t):
        return jnp.moveaxis(t.reshape(b, nc, SSM_CHUNK, *t.shape[2:]), 1, 0)

    xs = chunks(x.reshape(b, sp, SSM_GROUPS, hpg, SSM_HEAD_DIM))
    dts = chunks(dt.reshape(b, sp, SSM_GROUPS, hpg))
    bs, cs = chunks(bm), chunks(cm)
    a_g = a.reshape(SSM_GROUPS, hpg)
    mask = jnp.tril(jnp.ones((SSM_CHUNK, SSM_CHUNK), dtype=bool))[None, :, :, None, None]

    def step(state, inp):
        xc, dtc, bc, cc = inp
        cum = jnp.cumsum(dtc * a_g, axis=1)
        seg = cum[:, :, None] - cum[:, None, :]
        decay = jnp.exp(jnp.where(mask, seg, -jnp.inf))
        cb = jnp.einsum('bign,bjgn->bijg', cc, bc)
        w = cb[..., None] * decay * dtc[:, None]
        y = jnp.einsum('bijgh,bjghp->bighp', w, xc)
        y = y + jnp.einsum('bign,bghpn->bighp', cc, state) * jnp.exp(cum)[..., None]
        to_end = jnp.exp(cum[:, -1:] - cum) * dtc
        state = (state * jnp.exp(cum[:, -1])[..., None, None]
                 + jnp.einsum('bjgh,bjgn,bjghp->bghpn', to_end, bc, xc))
        return state, y

    state0 = jnp.zeros((b, SSM_GROUPS, hpg, SSM_HEAD_DIM, SSM_STATE), jnp.float32)
    _, ys = lax.scan(step, state0, (xs, dts, bs, cs))
    return jnp.moveaxis(ys, 0, 1).reshape(b, sp, SSM_HEADS, SSM_HEAD_DIM)[:, :s]


def mamba2_bidirectional(h, w_in, conv_w, conv_b, dt_bias, a_log, d_skip, norm_g, w_out):
    b, s, _ = h.shape
    proj = h @ w_in
    z, xbc, dt = jnp.split(proj, [SSM_INNER, SSM_INNER + SSM_CONV_CH], axis=-1)
    xbc = jax.nn.silu(depthwise_conv_centred(xbc, conv_w, conv_b))
    xs, bm, cm = jnp.split(xbc, [SSM_INNER, SSM_INNER + SSM_GROUPS * SSM_STATE], axis=-1)
    f32 = jnp.float32
    xs = xs.reshape(b, s, SSM_HEADS, SSM_HEAD_DIM).astype(f32)
    bm = bm.reshape(b, s, SSM_GROUPS, SSM_STATE).astype(f32)
    cm = cm.reshape(b, s, SSM_GROUPS, SSM_STATE).astype(f32)
    dt = jax.nn.softplus(dt.astype(f32) + dt_bias.astype(f32).reshape(2 * SSM_HEADS))
    dt_f, dt_b = dt[..., :SSM_HEADS], dt[..., SSM_HEADS:]
    a = -jnp.exp(a_log.astype(f32))
    flip = lambda t: jnp.flip(t, axis=1)
    y_f = ssd_scan(xs, dt_f, a[0], bm, cm)
    y_b = flip(ssd_scan(flip(xs), flip(dt_b), a[1], flip(bm), flip(cm)))
    y = y_f + y_b + d_skip.astype(f32)[:, None] * xs
    y = y.reshape(b, s, SSM_INNER) * jax.nn.silu(z.astype(f32))
    yg = y.reshape(b, s, SSM_GROUPS, SSM_INNER // SSM_GROUPS)
    yg = yg * lax.rsqrt(jnp.mean(yg * yg, axis=-1, keepdims=True) + EPS)
    y = (yg.reshape(b, s, SSM_INNER) * norm_g.astype(f32)).astype(h.dtype)
    return y @ w_out


def squared_relu_mlp(h, w1, w2):
    return jnp.square(jax.nn.relu(h @ w1)) @ w2


def setup_inputs(seed: int = 0) -> dict:
    key = jax.random.key(seed)
    ks = jax.random.split(key, 32)
    f32 = jnp.float32

    def nrm(k, shape, scale):
        return jax.random.normal(k, shape, f32) * scale

    def gain(k, shape):
        return 1.0 + 0.02 * jax.random.normal(k, shape, f32)

    x = nrm(ks[0], (BATCH, SEQ, D_MODEL), 1.0)
    c = nrm(ks[1], (BATCH, D_MODEL), 1.0)
    offset = jax.random.randint(ks[2], (BATCH, 1), 0, 1024, dtype=jnp.int32)
    positions = jnp.arange(SEQ, dtype=jnp.int32)[None, :] + offset
    dt0 = jnp.exp(jax.random.uniform(ks[21], (N_ODD, 2, SSM_HEADS), f32,
                                     math.log(1e-3), math.log(1e-1)))
    return {
        'x': x,
        'c': c,
        'positions': positions,
        'ada_w': nrm(ks[3], (DEPTH, D_MODEL, 6 * D_MODEL), 0.5 * D_MODEL ** -0.5),
        'ada_b': nrm(ks[4], (DEPTH, 6 * D_MODEL), 0.02),
        'norm_mix_g': gain(ks[5], (DEPTH, D_MODEL)),
        'norm_ffn_g': gain(ks[6], (DEPTH, D_MODEL)),
        'ffn_w1': nrm(ks[7], (DEPTH, D_MODEL, FFN_HIDDEN), D_MODEL ** -0.5),
        'ffn_w2': nrm(ks[8], (DEPTH, FFN_HIDDEN, D_MODEL), FFN_HIDDEN ** -0.5),
        'hyb_w_in': nrm(ks[9], (N_EVEN, D_MODEL, HYB_IN), D_MODEL ** -0.5),
        'sgu_norm_g': gain(ks[10], (N_EVEN, SGU_WIDTH)),
        'sgu_w': nrm(ks[11], (N_EVEN, SGU_GROUPS, SGU_CHUNK, SGU_CHUNK), SGU_CHUNK ** -0.5),
        'sgu_b': gain(ks[12], (N_EVEN, SGU_GROUPS, SGU_CHUNK)),
        'mla_q_norm_g': gain(ks[13], (N_EVEN, MLA_Q_RANK)),
        'mla_kv_norm_g': gain(ks[14], (N_EVEN, MLA_KV_RANK)),
        'mla_w_uq': nrm(ks[15], (N_EVEN, MLA_Q_RANK, MLA_HEADS * (MLA_NOPE + MLA_ROPE)), MLA_Q_RANK ** -0.5),
        'mla_w_ukv': nrm(ks[16], (N_EVEN, MLA_KV_RANK, MLA_HEADS * (MLA_NOPE + MLA_V)), MLA_KV_RANK ** -0.5),
        'hyb_w_out': nrm(ks[17], (N_EVEN, HYB_MIX, D_MODEL), HYB_MIX ** -0.5),
        'ssm_w_in': nrm(ks[18], (N_ODD, D_MODEL, SSM_IN), D_MODEL ** -0.5),
        'ssm_conv_w': nrm(ks[19], (N_ODD, SSM_CONV, SSM_CONV_CH), SSM_CONV ** -0.5),
        'ssm_conv_b': nrm(ks[20], (N_ODD, SSM_CONV_CH), 0.02),
        'ssm_dt_bias': dt0 + jnp.log(-jnp.expm1(-dt0)),
        'ssm_a_log': jnp.log(jax.random.uniform(ks[22], (N_ODD, 2, SSM_HEADS), f32, 1.0, 16.0)),
        'ssm_d': 1.0 + 0.1 * jax.random.normal(ks[23], (N_ODD, SSM_HEADS), f32),
        'ssm_norm_g': gain(ks[24], (N_ODD, SSM_INNER)),
        'ssm_w_out': nrm(ks[25], (N_ODD, SSM_INNER, D_MODEL), SSM_INNER ** -0.5),
        'final_norm_g': gain(ks[26], (D_MODEL,)),
    }


def reference(x, c, positions, ada_w, ada_b, norm_mix_g, norm_ffn_g, ffn_w1, ffn_w2,
              hyb_w_in, sgu_norm_g, sgu_w, sgu_b, mla_q_norm_g, mla_kv_norm_g, mla_w_uq,
              mla_w_ukv, hyb_w_out, ssm_w_in, ssm_conv_w, ssm_conv_b, ssm_dt_bias,
              ssm_a_log, ssm_d, ssm_norm_g, ssm_w_out, final_norm_g):
    cond = jax.nn.silu(c)
    for l in range(DEPTH):
        mod = (cond @ ada_w[l] + ada_b[l])[:, None, :]
        sh1, sc1, g1, sh2, sc2, g2 = jnp.split(mod, 6, axis=-1)
        h = rmsnorm(x, norm_mix_g[l]) * (1 + sc1) + sh1
        i = l // 2
        if l % 2 == 0:
            m = hybrid_mixer(h, positions, hyb_w_in[i], sgu_norm_g[i], sgu_w[i], sgu_b[i],
                             mla_q_norm_g[i], mla_kv_norm_g[i], mla_w_uq[i], mla_w_ukv[i],
                             hyb_w_out[i])
        else:
            m = mamba2_bidirectional(h, ssm_w_in[i], ssm_conv_w[i], ssm_conv_b[i],
                                     ssm_dt_bias[i], ssm_a_log[i], ssm_d[i],
                                     ssm_norm_g[i], ssm_w_out[i])
        x = x + g1 * m
        h = rmsnorm(x, norm_ffn_g[l]) * (1 + sc2) + sh2
        x = x + g2 * squared_relu_mlp(h, ffn_w1[l], ffn_w2[l])
    return rmsnorm(x, final_norm_g)
```

```python
class T:
    __slots__ = ("name", "w", "r")

    def __init__(self, name=""):
        self.name = name
        self.w = None
        self.r = []


class _Op:
    __slots__ = ("eng", "dma", "fn", "deps", "sig", "sem", "val", "idx", "bar")

    def __init__(self, eng, dma, fn):
        self.eng = eng
        self.dma = dma
        self.fn = fn
        self.deps = []
        self.sig = False
        self.sem = None
        self.val = 0
        self.bar = None


ENGS = ("pe", "act", "dve", "pool", "sp")


class Sched:
    NDMA = {"sp": 6, "pool": 6, "act": 2}

    NCC = 4

    def __init__(self, nc, es):
        self.nc = nc
        self.es = es
        self.ops = []
        self.tiles = []
        self.csem = {e: es.enter_context(nc.semaphore(f"c_{e}")) for e in ("pe", "act", "dve", "pool")}
        self.dsem = {e: [es.enter_context(nc.semaphore(f"d_{e}{i}")) for i in range(n)] for e, n in self.NDMA.items()}
        self.ccsem = [es.enter_context(nc.semaphore(f"cc{i}")) for i in range(self.NCC)]
        self.ccount = {e: 0 for e in self.csem}
        self.dcount = {e: [0] * n for e, n in self.NDMA.items()}
        self.cccount = [0] * self.NCC
        self.drr = {e: 0 for e in self.NDMA}
        self.ccrr = 0
        self.last_compute = {}

    def tile(self, name=""):
        t = T(name)
        self.tiles.append(t)
        return t

    def tiles_n(self, n, name=""):
        return [self.tile(f"{name}{i}") for i in range(n)]

    def _add(self, eng, dma, reads, writes, fn):
        op = _Op(eng, dma, fn)
        deps = set()
        for t in list(reads) + list(writes):
            if t.w is not None:
                deps.add(t.w)
        for t in writes:
            for r in t.r:
                deps.add(r)
        for d in deps:
            if d is op:
                continue
            if (not dma) and (not d.dma) and d.eng == eng:
                if eng == "pe":
                    continue
                if not any((t.w is d) for t in reads):
                    continue
            op.deps.append(d)
        for t in reads:
            t.r.append(op)
        for t in writes:
            t.w = op
            t.r = []
        self.ops.append(op)
        return op

    def op(self, eng, reads, writes, fn):
        return self._add(eng, False, reads, writes, fn)

    def dma(self, eng, reads, writes, fn):
        return self._add(eng, True, reads, writes, fn)

    def cc(self, reads, writes, fn):
        op = self._add("pool", True, reads, writes, fn)
        op.bar = "cc"
        return op

    def barrier(self):
        op = _Op("bar", False, None)
        self.ops.append(op)
        for t in self.tiles:
            t.w = None
            t.r = []
        return op

    def finish(self):
        self.flush()

    def flush(self):
        nc = self.nc
        self.barrier()
        ops = self.ops
        self.ops = []
        csem, dsem, ccsem = self.csem, self.dsem, self.ccsem
        for op in ops:
            if op.eng == "bar":
                for o in self.last_compute.values():
                    o.sig = True
                continue
            for d in op.deps:
                d.sig = True
            if not op.dma:
                self.last_compute[op.eng] = op
        for op in ops:
            if op.eng == "bar":
                op.bar = (dict(self.last_compute_snapshot(op, ops)), {k: list(v) for k, v in self.dcount.items()}, list(self.cccount))
                continue
            if op.dma and op.bar == "cc":
                i = self.ccrr
                self.ccrr = (i + 1) % self.NCC
                self.cccount[i] += 1
                op.sem = ccsem[i]
                op.val = self.cccount[i]
            elif op.dma:
                i = self.drr[op.eng]
                self.drr[op.eng] = (i + 1) % self.NDMA[op.eng]
                self.dcount[op.eng][i] += 16
                op.sem = dsem[op.eng][i]
                op.val = self.dcount[op.eng][i]
            elif op.sig and op.sem is None:
                self.ccount[op.eng] += 1
                op.sem = csem[op.eng]
                op.val = self.ccount[op.eng]
        per = {e: [] for e in ENGS}
        for op in ops:
            if op.eng == "bar":
                for e in ENGS:
                    per[e].append(op)
            else:
                per[op.eng].append(op)
        with nc.Block() as blk:
            def run(ename, eng):
                seen = {}

                def wait(sem, val):
                    if val <= 0:
                        return
                    k = id(sem)
                    if seen.get(k, 0) >= val:
                        return
                    seen[k] = val
                    eng.wait_ge(sem, val)

                for op in per[ename]:
                    if op.eng == "bar":
                        lc, dc, cc = op.bar
                        for o in lc.values():
                            wait(o.sem, o.val)
                        for e, vals in dc.items():
                            for i, v in enumerate(vals):
                                wait(dsem[e][i], v)
                        for i, v in enumerate(cc):
                            wait(ccsem[i], v)
                        continue
                    for d in op.deps:
                        wait(d.sem, d.val)
                    if op.dma and op.bar == "cc":
                        if op.val > 1:
                            wait(op.sem, op.val - 1)
                        op.fn(eng).then_inc(op.sem)
                        continue
                    if op.dma and op.val > 16:
                        wait(op.sem, op.val - 16)
                    ins = op.fn(eng)
                    if op.dma:
                        ins.then_inc(op.sem, 16)
                    elif op.sig:
                        ins.then_inc(op.sem, 1)

            @blk.tensor
            def _(e):
                run("pe", e)

            @blk.scalar
            def _(e):
                run("act", e)

            @blk.vector
            def _(e):
                run("dve", e)

            @blk.gpsimd
            def _(e):
                run("pool", e)

            @blk.sync
            def _(e):
                run("sp", e)

    def last_compute_snapshot(self, bar_op, ops):
        snap = dict(getattr(self, "_lc_prev", {}))
        for op in ops:
            if op is bar_op:
                break
            if op.eng != "bar" and not op.dma:
                snap[op.eng] = op
        if bar_op is ops[-1]:
            self._lc_prev = dict(snap)
        return snap

import numpy as np
from contextlib import ExitStack
import concourse.bass as bass
import concourse.mybir as mybir
from concourse.bass_utils import run_bass_kernel_spmd

F32 = mybir.dt.float32
BF16 = mybir.dt.bfloat16
I32 = mybir.dt.int32
AF = mybir.ActivationFunctionType
ALU = mybir.AluOpType
AX = mybir.AxisListType
EPS = 1e-6
NCORES = 8


class Cx:
    def __init__(self, name="k"):
        self.nc = bass.Bass("TRN2", target_bir_lowering=False)
        self.es = ExitStack()
        self.S = Sched(self.nc, self.es)
        self.nps = 0
        self.ps = []
        self.psT = []
        for i in range(8):
            p = self.es.enter_context(self.nc.psum_tensor(f"ps{i}", [128, 512], F32))
            self.ps.append(p)
            self.psT.append(self.S.tile(f"ps{i}"))
        self.uid = 0
        self.pes = None
        self.ones_bf, self.ones_bfT = self.sb("ones_bf", [128, 128], BF16)
        self.S.op("dve", [], [self.ones_bfT], lambda e: e.memset(self.ones_bf[:], 1.0))
        self._keep = set(self.S.tiles)

    def din(self, name, shape, dt=F32):
        return self.nc.dram_tensor(name, list(shape), dt, kind="ExternalInput").ap()

    def dout(self, name, shape, dt=F32):
        return self.nc.dram_tensor(name, list(shape), dt, kind="ExternalOutput").ap()

    def sb(self, name, shape, dt, es=None):
        self.uid += 1
        t = (es or getattr(self, "pes", None) or self.es).enter_context(self.nc.sbuf_tensor(f"{name}_{self.uid}", list(shape), dt))
        return t, self.S.tile(name)

    def dint(self, name, shape, dt=F32):
        return self.nc.dram_tensor(name, list(shape), dt).ap()

    def phase_begin(self):
        self.pes = ExitStack()

    def phase_end(self):
        self.S.flush()
        self.pes.close()
        self.pes = None
        self.S.tiles = [t for t in self.S.tiles if t in self._keep]

    def next_ps(self, n=1):
        if self.nps + n > 6:
            self.nps = 0
        r = list(range(self.nps, self.nps + n))
        self.nps = (self.nps + n) % 6
        return r

    def finish(self):
        self.S.finish()
        self.es.close()
        return self.nc


def load_small(cx, dram_ap, shape, dt=F32, name="c", eng="sp"):
    t, tT = cx.sb(name, shape, dt)
    cx.S.dma(eng, [], [tT], lambda e: e.dma_start(out=t[:], in_=dram_ap))
    return t, tT


def gemm_T(cx, w_ap, K, cols, rhs_fn, nt, evac_fn, wbufs, MB=256):
    KC = K // 128
    S = cx.S
    MB = min(MB, (4096 // KC) // 128 * 128) if KC * 128 <= 4096 else 128
    groups = []
    cur = []
    for ci, (c0, n) in enumerate(cols):
        if cur and (c0 == cur[-1][1] + cur[-1][2]) and (c0 + n - cur[0][1] <= MB):
            cur.append((ci, c0, n))
        else:
            if cur:
                groups.append(cur)
            cur = [(ci, c0, n)]
    if cur:
        groups.append(cur)
    if not hasattr(cx, "_wrr"):
        cx._wrr = 0
    for g in groups:
        g0 = g[0][1]
        gw = g[-1][1] + g[-1][2] - g0
        wb, wbT = wbufs[cx._wrr % len(wbufs)]
        cx._wrr += 1
        src = w_ap[:, g0:g0 + gw].rearrange("(c p) m -> p c m", p=128)
        wb = wb[:, 0:KC * gw].rearrange("p (c m) -> p c m", m=gw)
        S.dma("pool", [], [wbT], lambda e, wb=wb, src=src: e.dma_start(out=wb, in_=src))
        for (ci, c0, n) in g:
            off = c0 - g0
            for t in range(nt):
                (b,) = cx.next_ps(1)
                ps = cx.ps[b]
                rr = [rhs_fn(k, t) for k in range(KC)]

                def mm(e, ps=ps, wb=wb, off=off, n=n, rr=rr):
                    ins = None
                    for k in range(KC):
                        ins = e.matmul(ps[0:n, :], wb[:, k, off:off + n], rr[k][0], start=(k == 0), stop=(k == KC - 1))
                    return ins
                S.op("pe", [wbT] + list({id(r[1]): r[1] for r in rr}.values()), [cx.psT[b]], mm)
                evac_fn(ci, t, ps[0:n, :], cx.psT[b])


def sumsq_bcast(cx, sq_fn, nchunks, ps_b):
    rr = [sq_fn(c) for c in range(nchunks)]
    ps = cx.ps[ps_b]

    def mm(e):
        ins = None
        for c in range(nchunks):
            ins = e.matmul(ps[:, :], cx.ones_bf[:, :], rr[c][0], start=(c == 0), stop=(c == nchunks - 1))
        return ins
    cx.S.op("pe", [cx.ones_bfT] + list({id(r[1]): r[1] for r in rr}.values()), [cx.psT[ps_b]], mm)


def rstd_from_ps(cx, ps_b, out_ap, outT, D):
    ps = cx.ps[ps_b]
    cx.S.op("act", [cx.psT[ps_b]], [outT], lambda e: e.activation(out=out_ap, in_=ps[:, :], func=AF.Sqrt, scale=1.0 / D, bias=EPS))
    cx.S.op("dve", [outT], [outT], lambda e: e.reciprocal(out_ap, out_ap))

def mod_cols(cx, mod_ap, name="mod"):
    if len(mod_ap.shape) == 3:
        t, tT = cx.sb(name, [128, 96], F32)
        cx.S.dma("sp", [], [tT], lambda e: e.dma_start(out=t[:].rearrange("p (r j) -> p r j", j=12), in_=mod_ap))
        return t, tT
    return load_small(cx, mod_ap, [128, 96], F32, name)


def norm_T(cx, x_fn, nchunks, ntok, D, sqbufs, rstd, rstdT):
    S = cx.S
    nt = ntok // 512
    banks = [6, 7][:nt]
    for c in range(nchunks):
        xa, xT = x_fn(c)
        sq, sqT = sqbufs[c % len(sqbufs)]
        S.op("act", [xT], [sqT], lambda e, sq=sq, xa=xa: e.activation(out=sq[:, 0:ntok], in_=xa, func=AF.Square))
        for t in range(nt):
            ps = cx.ps[banks[t]]
            S.op("pe", [cx.ones_bfT, sqT], [cx.psT[banks[t]]],
                 lambda e, ps=ps, sq=sq, t=t, c=c: e.matmul(ps[:, :], cx.ones_bf[:, :], sq[:, t * 512:(t + 1) * 512], start=(c == 0), stop=(c == nchunks - 1)))
    for t in range(nt):
        rstd_from_ps(cx, banks[t], rstd[:, t * 512:(t + 1) * 512], rstdT, D)


def phase_C(cx, NT, KM, xT_ap, mixT_ap, w_out_ap, mod_ap, nfg_ap, w1_ap, w2_ap, mode, xo_ap, nmod_ap=None, ng_ap=None, ho_ap=None, gather=None):
    S = cx.S
    D = 2048
    TS = 1024
    KMC = KM // 128
    mod, modT = mod_cols(cx, mod_ap)
    nfg, nfgT = load_small(cx, nfg_ap, [128, 16], F32, "nfg")
    gsc, gscT = cx.sb("gsc", [128, 16], F32)
    S.op("dve", [modT, nfgT], [gscT], lambda e: e.scalar_tensor_tensor(gsc[:], mod[:, 64:80], 1.0, nfg[:], op0=ALU.add, op1=ALU.mult))
    if mode == "mid":
        nmod, nmodT = mod_cols(cx, nmod_ap, "nmod")
        ng, ngT = load_small(cx, ng_ap, [128, 16], F32, "ng")
        gsn, gsnT = cx.sb("gsn", [128, 16], F32)
        S.op("dve", [nmodT, ngT], [gsnT], lambda e: e.scalar_tensor_tensor(gsn[:], nmod[:, 16:32], 1.0, ng[:], op0=ALU.add, op1=ALU.mult))
    else:
        ng, ngT = load_small(cx, ng_ap, [128, 16], F32, "fg")
    x1, _ = cx.sb("x1", [128, 16, TS], F32)
    x1T = cx.S.tiles_n(16, "x1")
    h2, _ = cx.sb("h2", [128, 16, TS], BF16)
    h2T = cx.S.tiles_n(16, "h2")
    shr, _ = cx.sb("shr", [128, 32, TS], BF16)
    shrT = cx.S.tiles_n(32, "shr")
    wbufs = [cx.sb("wb", [128, 4096], BF16) for _ in range(2 if gather is not None else 3)]
    sqbufs = [cx.sb("sq", [128, TS], BF16) for _ in range(2)]
    rstd, rstdT = cx.sb("rstd", [128, TS], F32)
    tmps = [cx.sb("tmp", [128, 512], F32) for _ in range(3)]
    trr = [0]

    def tmp():
        trr[0] += 1
        return tmps[trr[0] % 3]

    def do_st(st):
        c0 = st * TS
        for j in range(4):
            S.dma("sp", [], x1T[4 * j:4 * j + 4], lambda e, j=j: e.dma_start(
                out=x1[:, 4 * j:4 * j + 4, :], in_=xT_ap[512 * j:512 * (j + 1), c0:c0 + TS].rearrange("(c p) n -> p c n", p=128)))
        if gather is None:
            for j in range(KMC // 8):
                src = mixT_ap[j] if isinstance(mixT_ap, (list, tuple)) else mixT_ap[1024 * j:1024 * (j + 1), :]
                S.dma("sp", [], shrT[8 * j:8 * j + 8], lambda e, j=j, src=src: e.dma_start(
                    out=shr[:, 8 * j:8 * j + 8, :], in_=src[:, c0:c0 + TS].rearrange("(c p) n -> p c n", p=128)))
        else:
            yg_ap, ixt, ixT, ident, identT, ytm = gather
            for tb in range(TS // 128):
                col0 = (st * (TS // 128) + tb) * 8
                for g in range(8):
                    Y_, YT_ = ytm[(tb * 8 + g) % len(ytm)]
                    S.dma("pool", [ixT], [YT_], lambda e, Y_=Y_, col=col0 + g: e.indirect_dma_start(
                        out=Y_[:], out_offset=None, in_=yg_ap, in_offset=bass.IndirectOffsetOnAxis(ap=ixt[:, col:col + 1], axis=0)))
                    (b,) = cx.next_ps(1)

                    def tr(e, b=b, Y_=Y_):
                        ins = None
                        for c in range(4):
                            ins = e.transpose(cx.ps[b][:, c * 128:(c + 1) * 128], Y_[:, c * 128:(c + 1) * 128], ident[:])
                        return ins
                    S.op("pe", [YT_, identT], [cx.psT[b]], tr)
                    eng = "act" if (g % 2 == 0) else "dve"
                    if eng == "act":
                        S.op("act", [cx.psT[b]], shrT[4 * g:4 * g + 4], lambda e, b=b, g=g, tb=tb: e.activation(
                            out=shr[:, 4 * g:4 * g + 4, tb * 128:(tb + 1) * 128], in_=cx.ps[b][:, :].rearrange("p (c n) -> p c n", n=128), func=AF.Copy))
                    else:
                        S.op("dve", [cx.psT[b]], shrT[4 * g:4 * g + 4], lambda e, b=b, g=g, tb=tb: e.tensor_copy(
                            shr[:, 4 * g:4 * g + 4, tb * 128:(tb + 1) * 128], cx.ps[b][:, :].rearrange("p (c n) -> p c n", n=128)))
        def ev1(ci, t, ps, psT, gcol=32):
            sl = slice(t * 512, (t + 1) * 512)
            S.op("dve", [psT, x1T[ci], modT], [x1T[ci]], lambda e: e.scalar_tensor_tensor(
                x1[:, ci, sl], ps, mod[:, gcol + ci:gcol + ci + 1], x1[:, ci, sl], op0=ALU.mult, op1=ALU.add))
        gemm_T(cx, w_out_ap, KM, [(m * 128, 128) for m in range(16)],
               lambda k, t: (shr[:, k, t * 512:(t + 1) * 512], shrT[k]), 2, ev1, wbufs)
        norm_T(cx, lambda c: (x1[:, c, :], x1T[c]), 16, TS, D, sqbufs, rstd, rstdT)
        for c in range(16):
            for t in range(2):
                sl = slice(t * 512, (t + 1) * 512)
                tp, tpT = tmp()
                S.op("dve", [x1T[c], rstdT, gscT], [tpT], lambda e, c=c, sl=sl, tp=tp: e.scalar_tensor_tensor(
                    tp[:], x1[:, c, sl], gsc[:, c:c + 1], rstd[:, sl], op0=ALU.mult, op1=ALU.mult))
                S.op("act", [tpT, modT], [h2T[c]], lambda e, c=c, sl=sl, tp=tp: e.activation(
                    out=h2[:, c, sl], in_=tp[:], func=AF.Identity, bias=mod[:, 48 + c:49 + c]))
        for hh in range(2):
            def ev2(ci, t, ps, psT):
                sl = slice(t * 512, (t + 1) * 512)
                tp, tpT = tmp()
                S.op("act", [psT], [tpT], lambda e: e.activation(out=tp[:], in_=ps, func=AF.Relu))
                S.op("dve", [tpT], [shrT[ci]], lambda e: e.tensor_tensor(shr[:, ci, sl], tp[:], tp[:], op=ALU.mult))
            gemm_T(cx, w1_ap, D, [(hh * 4096 + m * 128, 128) for m in range(32)],
                   lambda k, t: (h2[:, k, t * 512:(t + 1) * 512], h2T[k]), 2, ev2, wbufs)
            gemm_T(cx, w2_ap[hh * 4096:(hh + 1) * 4096, :], 4096, [(m * 128, 128) for m in range(16)],
                   lambda k, t: (shr[:, k, t * 512:(t + 1) * 512], shrT[k]), 2,
                   lambda ci, t, ps, psT: ev1(ci, t, ps, psT, gcol=80), wbufs)
        if mode == "mid":
            for j in range(4):
                S.dma("sp", x1T[4 * j:4 * j + 4], [], lambda e, j=j: e.dma_start(
                    out=xo_ap[512 * j:512 * (j + 1), c0:c0 + TS].rearrange("(c p) n -> p c n", p=128), in_=x1[:, 4 * j:4 * j + 4, :]))
        norm_T(cx, lambda c: (x1[:, c, :], x1T[c]), 16, TS, D, sqbufs, rstd, rstdT)
        for c in range(16):
            for t in range(2):
                sl = slice(t * 512, (t + 1) * 512)
                if mode == "mid":
                    tp, tpT = tmp()
                    S.op("dve", [x1T[c], rstdT, gsnT], [tpT], lambda e, c=c, sl=sl, tp=tp: e.scalar_tensor_tensor(
                        tp[:], x1[:, c, sl], gsn[:, c:c + 1], rstd[:, sl], op0=ALU.mult, op1=ALU.mult))
                    S.op("act", [tpT, nmodT], [h2T[c]], lambda e, c=c, sl=sl, tp=tp: e.activation(
                        out=h2[:, c, sl], in_=tp[:], func=AF.Identity, bias=nmod[:, c:c + 1]))
                else:
                    S.op("dve", [x1T[c], rstdT, ngT], [x1T[c]], lambda e, c=c, sl=sl: e.scalar_tensor_tensor(
                        x1[:, c, sl], x1[:, c, sl], ng[:, c:c + 1], rstd[:, sl], op0=ALU.mult, op1=ALU.mult))
        for j in range(4):
            if mode == "mid":
                S.dma("sp", h2T[4 * j:4 * j + 4], [], lambda e, j=j: e.dma_start(
                    out=ho_ap[512 * j:512 * (j + 1), c0:c0 + TS].rearrange("(c p) n -> p c n", p=128), in_=h2[:, 4 * j:4 * j + 4, :]))
            else:
                S.dma("sp", x1T[4 * j:4 * j + 4], [], lambda e, j=j: e.dma_start(
                    out=xo_ap[512 * j:512 * (j + 1), c0:c0 + TS].rearrange("(c p) n -> p c n", p=128), in_=x1[:, 4 * j:4 * j + 4, :]))

    for st_ in range(NT // TS):
        do_st(st_)

import ml_dtypes
_BF = ml_dtypes.bfloat16
_S, _D, _NT = 16384, 2048, 2048


def _lay(v):
    return np.ascontiguousarray(np.asarray(v, np.float32).reshape(-1, 128).T)


def _ca(a):
    return np.ascontiguousarray(a)


def _allgather(cx, src2d, dst2d):
    cx.phase_begin()
    rg = [list(range(NCORES))]
    cx.S.cc([], [], lambda e: e.collective_compute("AllGather", ALU.bypass, replica_groups=rg, ins=[src2d.opt()], outs=[dst2d.opt()]))
    cx.phase_end()


def build_program():
    S_, NT = _S, _NT
    U32 = mybir.dt.uint32
    cx = Cx()
    cx.S.flush()
    d = cx.din
    xT = d("xT", [2048, NT]); cT = d("cT", [128, 16]); adaw = d("adaw", [2, 2048, 1536]); adab = d("adab", [2, 128, 12])
    nmg0 = d("nmg0", [128, 16]); nmg1 = d("nmg1", [128, 16]); nfg0 = d("nfg0", [128, 16]); nfg1 = d("nfg1", [128, 16]); fng = d("fng", [128, 16])
    w_in0 = d("w_in0", [2048, 3136]); sgug = d("sgug", [1, 1024]); wsT = d("wsT", [128, 8, 128]); bs = d("bs", [1, 1024])
    qg = d("qg", [128, 4]); kvg = d("kvg", [128, 4]); w_uq = d("w_uq", [512, 1536]); w_ukv = d("w_ukv", [512, 2048])
    pos = d("pos", [1, NT], I32); invf = d("invf", [64, 1])
    w_out0 = d("w_out0", [2048, 2048]); w1_0 = d("w1_0", [2048, 8192]); w2_0 = d("w2_0", [8192, 2048])
    w1_1 = d("w1_1", [2048, 8192]); w2_1 = d("w2_1", [8192, 2048]); w_out1 = d("w_out1", [4096, 2048])
    wg = d("wg", [2048, 1296]); cw = d("cw", [128, 6, 5]); cb = d("cb", [128, 6]); dtb = d("dtb", [8, 2])
    alog_f = d("alog_f", [8, 1]); alog_b = d("alog_b", [8, 1]); dbc = d("dbc", [1, 512]); ngbc = d("ngbc", [1, 512])
    ix = d("ix", [128, 128], U32)
    outT = cx.dout("outT", [2048, NT])
    di = cx.dint
    mod_loc = di("mod_loc", [2, 128, 12]); mod_g = di("mod_g", [2048, 12])
    cx.phase_begin()
    phase_MOD(cx, cT, adaw, adab, mod_loc)
    cx.phase_end()
    _allgather(cx, mod_loc.rearrange("l p j -> (l p) j"), mod_g)
    modv = lambda l: mod_g.rearrange("(r l p) j -> l p r j", r=8, l=2)[l]
    aT = di("aT", [1024, NT], BF16); qT = di("qT", [8, 192, NT], BF16); kT = di("kT", [8, 128, NT], BF16)
    kpeT = di("kpeT", [64, NT], BF16); v = di("v", [NT, 1024], BF16)
    cx.phase_begin()
    phase_A0(cx, NT, xT, modv(0), nmg0, w_in0, sgug, wsT, bs, qg, kvg, w_uq, w_ukv, pos, invf, aT, qT, kT, kpeT, v)
    cx.phase_end()
    kT_g = di("kT_g", [8 * 1024, NT], BF16); kpeT_g = di("kpeT_g", [8 * 64, NT], BF16); v_g = di("v_g", [S_, 1024], BF16)
    _allgather(cx, kT.rearrange("h p n -> (h p) n"), kT_g)
    _allgather(cx, kpeT, kpeT_g)
    _allgather(cx, v, v_g)
    bT = di("bT", [1024, NT], BF16)
    cx.phase_begin()
    phase_B0(cx, NT, S_, qT, kT_g, kpeT_g, v_g, bT)
    cx.phase_end()
    x1T = di("x1T", [2048, NT]); ho = di("ho", [2048, NT], BF16)
    cx.phase_begin()
    phase_C(cx, NT, 2048, xT, [aT, bT], w_out0, modv(0), nfg0, w1_0, w2_0, "mid", x1T, nmod_ap=modv(1), ng_ap=nmg1, ho_ap=ho)
    cx.phase_end()
    hT_g = di("hT_g", [8 * 2048, NT], BF16)
    _allgather(cx, ho, hT_g)
    ztm = di("ztm", [S_, 512]); xraw = di("xraw", [768, S_]); xbc = di("xbc", [768, S_]); dt = di("dt", [2, 8, S_])
    cx.phase_begin()
    phase_A1a(cx, S_, lambda ti: hT_g[(ti // 4) * 2048:(ti // 4 + 1) * 2048, (ti % 4) * 512:(ti % 4 + 1) * 512], wg, cw, cb, dtb, ztm, xraw, xbc, dt)
    cx.phase_end()
    xtm = di("xtm", [S_, 512]); btm = di("btm", [S_, 128], BF16); yf = di("yf", [S_, 512]); yb = di("yb", [S_, 512])
    cx.phase_begin()
    phase_A1b2(cx, S_, xbc, xtm, btm, dt, alog_f, alog_b, yf, yb)
    cx.phase_end()
    yg = di("yg", [S_, 512], BF16); yg_g = di("yg_g", [8 * S_, 512], BF16)
    cx.phase_begin()
    phase_A1c(cx, S_, yf, yb, xtm, ztm, dbc, ngbc, yg)
    cx.phase_end()
    _allgather(cx, yg, yg_g)
    cx.phase_begin()
    ixt, ixT = load_small(cx, ix, [128, 128], U32, "ix")
    ident, identT = cx.sb("identc", [128, 128], F32)
    cx.S.op("dve", [], [identT], lambda e: e.memset(ident[:], 0.0))
    cx.S.op("pool", [identT], [identT], lambda e: e.affine_select(out=ident[:], in_=ident[:], pattern=[[-1, 128]], compare_op=ALU.not_equal, fill=1.0, base=0, channel_multiplier=1))
    ytm = [cx.sb("ytm", [128, 512], F32) for _ in range(2)]
    phase_C(cx, NT, 4096, x1T, None, w_out1, modv(1), nfg1, w1_1, w2_1, "final", outT, ng_ap=fng, gather=(yg_g, ixt, ixT, ident, identT, ytm))
    cx.phase_end()
    cx.es.close()
    return cx.nc


def make_inputs(x, c, positions, ada_w, ada_b, norm_mix_g, norm_ffn_g, ffn_w1, ffn_w2,
                hyb_w_in, sgu_norm_g, sgu_w, sgu_b, mla_q_norm_g, mla_kv_norm_g, mla_w_uq,
                mla_w_ukv, hyb_w_out, ssm_w_in, ssm_conv_w, ssm_conv_b, ssm_dt_bias,
                ssm_a_log, ssm_d, ssm_norm_g, ssm_w_out, final_norm_g):
    f32 = np.float32
    A = lambda a: np.asarray(a, f32)
    x = A(x); c = A(c); positions = np.asarray(positions, np.int32); ada_w = A(ada_w); ada_b = A(ada_b)
    S_, NT, NC = _S, _NT, NCORES
    half = 32
    inv_freq = (10000.0 ** (-np.arange(half, dtype=np.float32) / half)).astype(f32)
    w_in1 = A(ssm_w_in[0]); cw_ = A(ssm_conv_w[0]); cb_ = A(ssm_conv_b[0]); dtb_ = A(ssm_dt_bias[0]); alog_ = A(ssm_a_log[0])
    dsk = A(ssm_d[0]); ngm = A(ssm_norm_g[0])
    com = {"cT": _lay(c[0]), "nmg0": _lay(norm_mix_g[0]), "nmg1": _lay(norm_mix_g[1]), "nfg0": _lay(norm_ffn_g[0]), "nfg1": _lay(norm_ffn_g[1]),
           "fng": _lay(final_norm_g), "w_in0": _ca(A(hyb_w_in[0])), "sgug": _ca(A(sgu_norm_g[0])[None]),
           "wsT": _ca(A(sgu_w[0]).transpose(2, 0, 1)), "bs": _ca(A(sgu_b[0]).reshape(1, -1)), "qg": _lay(mla_q_norm_g[0]), "kvg": _lay(mla_kv_norm_g[0]),
           "w_uq": _ca(A(mla_w_uq[0])), "w_ukv": _ca(A(mla_w_ukv[0])), "invf": _ca(np.concatenate([inv_freq, inv_freq])[:, None].astype(f32)),
           "w_out0": _ca(A(hyb_w_out[0])), "w1_0": _ca(A(ffn_w1[0])), "w2_0": _ca(A(ffn_w2[0])), "w1_1": _ca(A(ffn_w1[1])), "w2_1": _ca(A(ffn_w2[1])),
           "w_out1": _ca(A(ssm_w_out[0]))}
    ims = []
    for i in range(NC):
        g = i
        tok = slice(i * NT, (i + 1) * NT)
        cs = slice(i * 1536, (i + 1) * 1536)
        cols = np.concatenate([np.arange(g * 512, (g + 1) * 512), 4096 + np.arange(g * 512, (g + 1) * 512),
                               8192 + np.arange(g * 128, (g + 1) * 128), 9216 + np.arange(g * 128, (g + 1) * 128),
                               10240 + np.arange(g * 8, (g + 1) * 8), 10304 + np.arange(g * 8, (g + 1) * 8)])
        ch = np.concatenate([np.arange(g * 512, (g + 1) * 512), 4096 + np.arange(g * 128, (g + 1) * 128), 5120 + np.arange(g * 128, (g + 1) * 128)])
        ixv = np.zeros((128, 128), np.uint32)
        for tb in range(16):
            for gg in range(8):
                ixv[:, tb * 8 + gg] = gg * S_ + i * NT + tb * 128 + np.arange(128)
        ims.append(dict(com, xT=_ca(x[0, tok].T), adaw=_ca(ada_w[:, :, cs]), adab=_ca(np.stack([_lay(ada_b[l, cs]) for l in range(2)])),
                        pos=_ca(positions[:, tok]), wg=_ca(w_in1[:, cols]), cw=_ca(cw_[:, ch].reshape(5, 6, 128).transpose(2, 1, 0)),
                        cb=_ca(cb_[ch].reshape(6, 128).T), dtb=_ca(dtb_[:, g * 8:(g + 1) * 8].T),
                        alog_f=_ca(alog_[0, g * 8:(g + 1) * 8][:, None]), alog_b=_ca(alog_[1, g * 8:(g + 1) * 8][:, None]),
                        dbc=_ca(np.repeat(dsk[g * 8:(g + 1) * 8], 64)[None]), ngbc=_ca(ngm[g * 512:(g + 1) * 512][None]), ix=ixv))
    return ims


def kernel(**inputs):
    ims = make_inputs(**inputs)
    nc = build_program()
    res = run_bass_kernel_spmd(nc, ims, core_ids=list(range(NCORES)))
    out = np.concatenate([res.results[i]["outT"].T for i in range(NCORES)], axis=0)[None]
    return _ca(out.astype(np.float32))

import math
ROPE_C1 = 6.28125
ROPE_C2 = 2 * math.pi - 6.28125


def bc_rows(ap, n):
    return ap.broadcast(0, n) if hasattr(ap, "broadcast") else ap[0:1, :].to_broadcast([n, ap.shape[1]])


def rope_tables(cx, pos_ap, c0, n, invf, invfT, bufs):
    S = cx.S
    (pt, ptT), (ang, angT), (kf, kfT), (ki, kiT), (cs, csT), (sn, snT), (a2, a2T) = bufs
    S.dma("sp", [], [ptT], lambda e: e.dma_start(out=pt[:, 0:n], in_=bc_rows(pos_ap[0:1, c0:c0 + n], 64)))
    S.op("dve", [ptT], [angT], lambda e: e.tensor_copy(ang[:, 0:n], pt[:, 0:n]))
    S.op("dve", [angT, invfT], [angT], lambda e: e.tensor_scalar(ang[:, 0:n], ang[:, 0:n], invf[:, 0:1], None, op0=ALU.mult))
    for (dst, dstT, shift) in ((sn, snT, 0.0), (cs, csT, math.pi / 2)):
        S.op("dve", [angT], [kfT], lambda e, shift=shift: e.tensor_scalar(kf[:, 0:n], ang[:, 0:n], 1.0 / (2 * math.pi), 0.5 + shift / (2 * math.pi), op0=ALU.mult, op1=ALU.add))
        S.op("dve", [kfT], [kiT], lambda e: e.tensor_copy(ki[:, 0:n], kf[:, 0:n]))
        S.op("dve", [kiT], [kfT], lambda e: e.tensor_copy(kf[:, 0:n], ki[:, 0:n]))
        S.op("dve", [kfT, angT], [a2T], lambda e: e.scalar_tensor_tensor(a2[:, 0:n], kf[:, 0:n], -ROPE_C1, ang[:, 0:n], op0=ALU.mult, op1=ALU.add))
        S.op("dve", [kfT, a2T], [a2T], lambda e, shift=shift: e.scalar_tensor_tensor(a2[:, 0:n], kf[:, 0:n], -ROPE_C2, a2[:, 0:n], op0=ALU.mult, op1=ALU.add))
        if shift:
            S.op("dve", [a2T], [a2T], lambda e, shift=shift: e.tensor_scalar(a2[:, 0:n], a2[:, 0:n], shift, None, op0=ALU.add))
        S.op("dve", [a2T], [kfT], lambda e: e.tensor_scalar(kf[:, 0:n], a2[:, 0:n], -math.pi, 2 * math.pi, op0=ALU.is_lt, op1=ALU.mult))
        S.op("dve", [kfT, a2T], [a2T], lambda e: e.tensor_tensor(a2[:, 0:n], a2[:, 0:n], kf[:, 0:n], op=ALU.add))
        S.op("dve", [a2T], [kfT], lambda e: e.tensor_scalar(kf[:, 0:n], a2[:, 0:n], math.pi, -2 * math.pi, op0=ALU.is_gt, op1=ALU.mult))
        S.op("dve", [kfT, a2T], [a2T], lambda e: e.tensor_tensor(a2[:, 0:n], a2[:, 0:n], kf[:, 0:n], op=ALU.add))
        S.op("act", [a2T], [dstT], lambda e, dst=dst: e.activation(out=dst[:, 0:n], in_=a2[:, 0:n], func=AF.Sin))


def phase_A0(cx, NT, xT_ap, mod_ap, nmg_ap, w_in_ap, sgug_ap, wsT_ap, bs_ap, qg_ap, kvg_ap, w_uq_ap, w_ukv_ap,
             pos_ap, invf_ap, aT_o, qT_o, kT_o, kpeT_o, v_o):
    S = cx.S
    D = 2048
    TS = 512
    mod, modT = mod_cols(cx, mod_ap)
    nmg, nmgT = load_small(cx, nmg_ap, [128, 16], F32, "nmg")
    gsc, gscT = cx.sb("gsc", [128, 16], F32)
    S.op("dve", [modT, nmgT], [gscT], lambda e: e.scalar_tensor_tensor(gsc[:], mod[:, 16:32], 1.0, nmg[:], op0=ALU.add, op1=ALU.mult))
    qg, qgT = load_small(cx, qg_ap, [128, 4], F32, "qg")
    kvg, kvgT = load_small(cx, kvg_ap, [128, 4], F32, "kvg")
    invf, invfT = load_small(cx, invf_ap, [64, 1], F32, "invf")
    gbc, gbcT = cx.sb("gbc", [128, 1024], F32)
    S.dma("sp", [], [gbcT], lambda e: e.dma_start(out=gbc[:], in_=bc_rows(sgug_ap, 128)))
    bsb, bsbT = cx.sb("bsb", [128, 1024], F32)
    S.dma("sp", [], [bsbT], lambda e: e.dma_start(out=bsb[:], in_=bc_rows(bs_ap, 128)))
    wsT, wsTT = cx.sb("wsT", [128, 8, 128], BF16)
    S.dma("pool", [], [wsTT], lambda e: e.dma_start(out=wsT[:], in_=wsT_ap))
    wuq, wuqT = cx.sb("wuq", [128, 4, 1536], BF16)
    S.dma("pool", [], [wuqT], lambda e: e.dma_start(out=wuq[:], in_=w_uq_ap.rearrange("(c p) m -> p c m", p=128)))
    wukv, wukvT = cx.sb("wukv", [128, 4, 2048], BF16)
    S.dma("pool", [], [wukvT], lambda e: e.dma_start(out=wukv[:], in_=w_ukv_ap.rearrange("(c p) m -> p c m", p=128)))
    wk, wkT = cx.sb("wk", [128, 16, 64], BF16)
    S.dma("pool", [], [wkT], lambda e: e.dma_start(out=wk[:], in_=w_in_ap[:, 3072:3136].rearrange("(c p) m -> p c m", p=128)))
    wkr, wkrT = cx.sb("wkr", [128, 16, 64], BF16)
    S.op("dve", [wkT], [wkrT], lambda e: e.tensor_scalar(wkr[:, :, 0:32], wk[:, :, 32:64], -1.0, None, op0=ALU.mult))
    S.op("dve", [wkT, wkrT], [wkrT], lambda e: e.tensor_copy(wkr[:, :, 32:64], wk[:, :, 0:32]))
    wuqr, wuqrT = cx.sb("wuqr", [128, 4, 8, 64], BF16)
    wuq4 = wuq[:].rearrange("p c (h m) -> p c h m", m=192)
    for kc in range(4):
        S.op("dve", [wuqT, wuqrT], [wuqrT], lambda e, kc=kc: e.tensor_scalar(wuqr[:, kc, :, 0:32], wuq4[:, kc, :, 160:192], -1.0, None, op0=ALU.mult))
        S.op("dve", [wuqT, wuqrT], [wuqrT], lambda e, kc=kc: e.tensor_copy(wuqr[:, kc, :, 32:64], wuq4[:, kc, :, 128:160]))
    wukv4 = wukv[:].rearrange("p c (h m) -> p c h m", m=256)

    xb, _ = cx.sb("xb", [128, 16, TS], F32)
    xbT = S.tiles_n(16, "xb")
    hT, _ = cx.sb("hT", [128, 16, TS], BF16)
    hTT = S.tiles_n(16, "hT")
    uT, _ = cx.sb("uT", [128, 8, TS], BF16)
    uTT = S.tiles_n(8, "uT")
    vtm, _ = cx.sb("vtm", [128, 4, 1024], BF16)
    vtmT = S.tiles_n(4, "vtm")
    lat, _ = cx.sb("lat", [128, 8, TS], BF16)
    latT = S.tiles_n(8, "lat")
    wbufs = [cx.sb("wb", [128, 4096], BF16) for _ in range(2)]
    sqbufs = [cx.sb("sq", [128, TS], BF16) for _ in range(2)]
    rstd, rstdT = cx.sb("rstd", [128, TS], F32)
    rq, rqT = cx.sb("rq", [128, TS], F32)
    rkv, rkvT = cx.sb("rkv", [128, TS], F32)
    tmps = [cx.sb("tmp", [128, 1024], F32) for _ in range(2)]
    vn, vnT = cx.sb("vn", [128, 1024], BF16)
    junk, junkT = cx.sb("junk", [128, 1024], BF16)
    st, stT = cx.sb("st", [128, 8], F32)
    ob, obT = cx.sb("ob", [128, 8, TS], BF16)
    qpeo, qpeoT = cx.sb("qpeo", [64, 8, TS], BF16)
    kpeo, kpeoT = cx.sb("kpeo", [64, TS], BF16)
    vo, _ = cx.sb("vo", [128, 4, 1024], BF16)
    voT = S.tiles_n(4, "vo")
    rb = [cx.sb(n_, [64, TS], I32 if n_ in ("pt", "ki") else F32) for n_ in ("pt", "ang", "kf", "ki", "cs", "sn", "a2")]
    cs, csT = rb[4]
    sn, snT = rb[5]
    r1, r1T = cx.sb("r1", [64, TS], F32)
    r2, r2T = cx.sb("r2", [64, TS], F32)
    trr = [0]

    def tmp():
        trr[0] += 1
        return tmps[trr[0] % 2]

    def rope_out(psA, psAT, psB, psBT, out_ap, outT):
        S.op("dve", [psAT, csT], [r1T], lambda e: e.tensor_tensor(r1[:], psA, cs[:], op=ALU.mult))
        S.op("dve", [psBT, snT], [r2T], lambda e: e.tensor_tensor(r2[:], psB, sn[:], op=ALU.mult))
        S.op("dve", [r1T, r2T], [outT], lambda e: e.tensor_tensor(out_ap, r1[:], r2[:], op=ALU.add))

    def do_tile(ti):
        c0 = ti * TS
        for j in range(4):
            S.dma("sp", [], xbT[4 * j:4 * j + 4], lambda e, j=j: e.dma_start(
                out=xb[:, 4 * j:4 * j + 4, :], in_=xT_ap[512 * j:512 * (j + 1), c0:c0 + TS].rearrange("(c p) n -> p c n", p=128)))
        rope_tables(cx, pos_ap, c0, TS, invf, invfT, rb)
        norm_T(cx, lambda c: (xb[:, c, :], xbT[c]), 16, TS, D, sqbufs, rstd, rstdT)
        for c in range(16):
            tp, tpT = tmp()
            S.op("dve", [xbT[c], rstdT, gscT], [tpT], lambda e, c=c, tp=tp: e.scalar_tensor_tensor(
                tp[:, 0:TS], xb[:, c, :], gsc[:, c:c + 1], rstd[:], op0=ALU.mult, op1=ALU.mult))
            S.op("act", [tpT, modT], [hTT[c]], lambda e, c=c, tp=tp: e.activation(
                out=hT[:, c, :], in_=tp[:, 0:TS], func=AF.Identity, bias=mod[:, c:c + 1]))
        rhs_h = lambda k, t: (hT[:, k, :], hTT[k])
        gemm_T(cx, w_in_ap, D, [(m * 128, 128) for m in range(8)], rhs_h, 1,
               lambda ci, t, ps, psT: S.op("act", [psT], [uTT[ci]], lambda e: e.activation(out=uT[:, ci, :], in_=ps, func=AF.Gelu_apprx_tanh)), wbufs)
        for cb in range(4):
            wb, wbT = wbufs[cx._wrr % len(wbufs)]
            cx._wrr += 1
            wv = wb[:, 0:16 * 256].rearrange("p (c m) -> p c m", m=256)
            S.dma("pool", [], [wbT], lambda e, wv=wv, cb=cb: e.dma_start(
                out=wv, in_=w_in_ap[:, 1024 + cb * 256:1024 + (cb + 1) * 256].rearrange("(c p) m -> p c m", p=128)))
            for tb in range(4):
                (b,) = cx.next_ps(1)
                ps = cx.ps[b]

                def mm(e, ps=ps, wv=wv, tb=tb):
                    ins = None
                    for k in range(16):
                        ins = e.matmul(ps[:, 0:256], hT[:, k, tb * 128:(tb + 1) * 128], wv[:, k, :], start=(k == 0), stop=(k == 15))
                    return ins
                S.op("pe", [wbT] + hTT, [cx.psT[b]], mm)
                S.op("act", [cx.psT[b]], [vtmT[tb]], lambda e, ps=ps, tb=tb, cb=cb: e.activation(
                    out=vtm[:, tb, cb * 256:(cb + 1) * 256], in_=ps[:, 0:256], func=AF.Gelu_apprx_tanh))
        for tb in range(4):
            S.op("dve", [], [stT], lambda e: e.memset(st[:], 0.0))
            S.op("act", [vtmT[tb], stT], [junkT, stT], lambda e, tb=tb: e.activation(out=junk[:], in_=vtm[:, tb, :], func=AF.Identity, accum_out=st[:, 0:1]))
            S.op("act", [vtmT[tb], stT], [junkT, stT], lambda e, tb=tb: e.activation(out=junk[:], in_=vtm[:, tb, :], func=AF.Square, accum_out=st[:, 1:2]))
            S.op("dve", [stT], [stT], lambda e: e.tensor_scalar(st[:, 2:4], st[:, 0:2], 1.0 / 1024, None, op0=ALU.mult))
            S.op("dve", [stT], [stT], lambda e: e.tensor_tensor(st[:, 4:5], st[:, 2:3], st[:, 2:3], op=ALU.mult))
            S.op("dve", [stT], [stT], lambda e: e.tensor_tensor(st[:, 5:6], st[:, 3:4], st[:, 4:5], op=ALU.subtract))
            S.op("act", [stT], [stT], lambda e: e.activation(out=st[:, 6:7], in_=st[:, 5:6], func=AF.Sqrt, bias=EPS))
            S.op("dve", [stT], [stT], lambda e: e.reciprocal(st[:, 7:8], st[:, 6:7]))
            tp, tpT = tmp()
            S.op("dve", [vtmT[tb], stT], [tpT], lambda e, tb=tb, tp=tp: e.tensor_scalar(
                tp[:], vtm[:, tb, :], st[:, 2:3], st[:, 7:8], op0=ALU.subtract, op1=ALU.mult))
            S.op("dve", [tpT, gbcT], [vnT], lambda e, tp=tp: e.tensor_tensor(vn[:], tp[:], gbc[:], op=ALU.mult))
            for half in range(2):
                (b,) = cx.next_ps(1)
                ps = cx.ps[b]

                def mm(e, ps=ps, half=half):
                    ins = None
                    for gg in range(4):
                        g = half * 4 + gg
                        ins = e.matmul(ps[:, gg * 128:(gg + 1) * 128], vn[:, g * 128:(g + 1) * 128], wsT[:, g, :], start=True, stop=True)
                    return ins
                S.op("pe", [vnT, wsTT], [cx.psT[b]], mm)
                tp, tpT = tmp()
                S.op("dve", [cx.psT[b], bsbT], [tpT], lambda e, ps=ps, tp=tp, half=half: e.tensor_tensor(
                    tp[:, 0:512], ps[:, :], bsb[:, half * 512:(half + 1) * 512], op=ALU.add))
                S.op("dve", [tpT] + uTT[half * 4:half * 4 + 4], uTT[half * 4:half * 4 + 4], lambda e, tp=tp, half=half, tb=tb: e.tensor_tensor(
                    uT[:, half * 4:half * 4 + 4, tb * 128:(tb + 1) * 128], tp[:, 0:512].rearrange("p (g i) -> p g i", i=128),
                    uT[:, half * 4:half * 4 + 4, tb * 128:(tb + 1) * 128], op=ALU.mult))
        S.dma("sp", uTT, [], lambda e: e.dma_start(out=aT_o[:, c0:c0 + TS].rearrange("(c p) n -> p c n", p=128), in_=uT[:]))

        def ev_lat(ci, t, ps, psT):
            S.op("act", [psT], [latT[ci]], lambda e: e.activation(out=lat[:, ci, :], in_=ps, func=AF.Copy))
            sq, sqT = sqbufs[ci % 2]
            S.op("act", [psT], [sqT], lambda e: e.activation(out=sq[:, 0:TS], in_=ps, func=AF.Square))
            bk = 6 if ci < 4 else 7
            S.op("pe", [cx.ones_bfT, sqT], [cx.psT[bk]], lambda e: e.matmul(
                cx.ps[bk][:, :], cx.ones_bf[:, :], sq[:, 0:TS], start=(ci % 4 == 0), stop=(ci % 4 == 3)))
        gemm_T(cx, w_in_ap, D, [(2048 + m * 128, 128) for m in range(8)], rhs_h, 1, ev_lat, wbufs)
        rstd_from_ps(cx, 6, rq[:], rqT, 512)
        rstd_from_ps(cx, 7, rkv[:], rkvT, 512)
        for m in range(8):
            g_, gT_, r_, rT_ = (qg, qgT, rq, rqT) if m < 4 else (kvg, kvgT, rkv, rkvT)
            S.op("dve", [latT[m], gT_, rT_], [latT[m]], lambda e, m=m, g_=g_, r_=r_: e.scalar_tensor_tensor(
                lat[:, m, :], lat[:, m, :], g_[:, m % 4:m % 4 + 1], r_[:], op0=ALU.mult, op1=ALU.mult))
        (bA,) = cx.next_ps(1)
        (bB,) = cx.next_ps(1)

        def mmk(e, w, b):
            ins = None
            for k in range(16):
                ins = e.matmul(cx.ps[b][0:64, :], w[:, k, :], hT[:, k, :], start=(k == 0), stop=(k == 15))
            return ins
        S.op("pe", [wkT] + hTT, [cx.psT[bA]], lambda e: mmk(e, wk, bA))
        S.op("pe", [wkrT] + hTT, [cx.psT[bB]], lambda e: mmk(e, wkr, bB))
        rope_out(cx.ps[bA][0:64, :], cx.psT[bA], cx.ps[bB][0:64, :], cx.psT[bB], kpeo[:], kpeoT)
        S.dma("sp", [kpeoT], [], lambda e: e.dma_start(out=kpeT_o[:, c0:c0 + TS], in_=kpeo[:]))
        for h in range(8):
            (b,) = cx.next_ps(1)

            def mmq(e, b=b, h=h):
                ins = None
                for kc in range(4):
                    ins = e.matmul(cx.ps[b][:, :], wuq[:, kc, h * 192:h * 192 + 128], lat[:, kc, :], start=(kc == 0), stop=(kc == 3))
                return ins
            S.op("pe", [wuqT] + latT[0:4], [cx.psT[b]], mmq)
            S.op("act", [cx.psT[b]], [obT], lambda e, b=b, h=h: e.activation(out=ob[:, h, :], in_=cx.ps[b][:, :], func=AF.Copy))
            (bA,) = cx.next_ps(1)
            (bB,) = cx.next_ps(1)

            def mmp(e, b, w):
                ins = None
                for kc in range(4):
                    ins = e.matmul(cx.ps[b][0:64, :], w(kc), lat[:, kc, :], start=(kc == 0), stop=(kc == 3))
                return ins
            S.op("pe", [wuqT] + latT[0:4], [cx.psT[bA]], lambda e, bA=bA, h=h: mmp(e, bA, lambda kc: wuq[:, kc, h * 192 + 128:h * 192 + 192]))
            S.op("pe", [wuqrT] + latT[0:4], [cx.psT[bB]], lambda e, bB=bB, h=h: mmp(e, bB, lambda kc: wuqr[:, kc, h, :]))
            rope_out(cx.ps[bA][0:64, :], cx.psT[bA], cx.ps[bB][0:64, :], cx.psT[bB], qpeo[:, h, :], qpeoT)
        S.dma("sp", [obT], [], lambda e: e.dma_start(out=qT_o[:, 0:128, c0:c0 + TS].rearrange("h p n -> p h n"), in_=ob[:]))
        S.dma("sp", [qpeoT], [], lambda e: e.dma_start(out=qT_o[:, 128:192, c0:c0 + TS].rearrange("h p n -> p h n"), in_=qpeo[:]))
        for h in range(8):
            (b,) = cx.next_ps(1)

            def mmkn(e, b=b, h=h):
                ins = None
                for kc in range(4):
                    ins = e.matmul(cx.ps[b][:, :], wukv[:, kc, h * 256:h * 256 + 128], lat[:, 4 + kc, :], start=(kc == 0), stop=(kc == 3))
                return ins
            S.op("pe", [wukvT] + latT[4:8], [cx.psT[b]], mmkn)
            S.op("act", [cx.psT[b], obT], [obT], lambda e, b=b, h=h: e.activation(out=ob[:, h, :], in_=cx.ps[b][:, :], func=AF.Copy))
        S.dma("sp", [obT], [], lambda e: e.dma_start(out=kT_o[:, :, c0:c0 + TS].rearrange("h p n -> p h n"), in_=ob[:]))
        for tb in range(4):
            for half in range(2):
                (b,) = cx.next_ps(1)

                def mmv(e, b=b, tb=tb, half=half):
                    ins = None
                    for kc in range(4):
                        ins = e.matmul(cx.ps[b][:, :], lat[:, 4 + kc, tb * 128:(tb + 1) * 128], wukv4[:, kc, half * 4:half * 4 + 4, 128:256], start=(kc == 0), stop=(kc == 3))
                    return ins
                S.op("pe", [wukvT] + latT[4:8], [cx.psT[b]], mmv)
                S.op("dve", [cx.psT[b]], [voT[tb]], lambda e, b=b, tb=tb, half=half: e.tensor_copy(vo[:, tb, half * 512:(half + 1) * 512], cx.ps[b][:, :]))
        S.dma("sp", voT, [], lambda e: e.dma_start(out=v_o[c0:c0 + TS, :].rearrange("(t p) f -> p t f", p=128), in_=vo[:]))

    for ti_ in range(NT // TS):
        do_tile(ti_)

def phase_A1a(cx, SQ, hT_ap, wg_ap, cw_ap, cb_ap, dtb_ap, zs_o, xraw_o, xbc_o, dt_o):
    S = cx.S
    TS = 512
    wg, _ = cx.sb("wg", [128, 16, 1296], BF16)
    wgT = S.tiles_n(4, "wg")
    for j in range(4):
        S.dma("pool", [], [wgT[j]], lambda e, j=j: e.dma_start(out=wg[:, 4 * j:4 * j + 4, :], in_=wg_ap[512 * j:512 * (j + 1), :].rearrange("(c p) m -> p c m", p=128)))
    cw, cwT = load_small(cx, cw_ap, [128, 6, 5], F32, "cw")
    cb, cbT = load_small(cx, cb_ap, [128, 6], F32, "cb")
    dtb, dtbT = load_small(cx, dtb_ap, [8, 2], F32, "dtb")
    hb = [cx.sb("hb", [128, 16, TS], BF16) for _ in range(2)]
    zo = [cx.sb("zo", [128, 4, TS], F32) for _ in range(2)]
    xr = [cx.sb("xr", [128, 6, TS], F32) for _ in range(2)]
    dto = [cx.sb("dto", [8, 2, TS], F32) for _ in range(2)]
    e1, e1T = cx.sb("e1", [8, TS], F32)
    xrawT = S.tiles_n(SQ // TS, "xraw_dram")

    def loadH(ti):
        c0 = ti * TS
        H, HT = hb[ti % 2]
        hsrc = hT_ap(ti) if callable(hT_ap) else hT_ap[:, c0:c0 + TS]
        S.dma("sp", [], [HT], lambda e: e.dma_start(out=H[:], in_=hsrc.rearrange("(c p) n -> p c n", p=128)))

    def tile1(ti):
        c0 = ti * TS
        H, HT = hb[ti % 2]
        Z, ZT = zo[ti % 2]
        X, XT = xr[ti % 2]
        DT, DTT = dto[ti % 2]
        for tb in range(4):
            (b,) = cx.next_ps(1)

            def mmz(e, b=b, tb=tb):
                ins = None
                for k in range(16):
                    ins = e.matmul(cx.ps[b][:, :], H[:, k, tb * 128:(tb + 1) * 128], wg[:, k, 0:512], start=(k == 0), stop=(k == 15))
                return ins
            S.op("pe", wgT + [HT], [cx.psT[b]], mmz)
            S.op("act", [cx.psT[b]], [ZT], lambda e, b=b, tb=tb: e.activation(out=Z[:, tb, :], in_=cx.ps[b][:, :], func=AF.Silu))
        for m in range(4, 10):
            (b,) = cx.next_ps(1)

            def mm(e, b=b, m=m):
                ins = None
                for k in range(16):
                    ins = e.matmul(cx.ps[b][:, :], wg[:, k, m * 128:(m + 1) * 128], H[:, k, :], start=(k == 0), stop=(k == 15))
                return ins
            S.op("pe", wgT + [HT], [cx.psT[b]], mm)
            S.op("dve", [cx.psT[b]], [XT], lambda e, b=b, m=m: e.tensor_copy(X[:, m - 4, :], cx.ps[b][:, :]))
        for d in range(2):
            (b,) = cx.next_ps(1)

            def mmd(e, b=b, d=d):
                ins = None
                for k in range(16):
                    ins = e.matmul(cx.ps[b][0:8, :], wg[:, k, 1280 + 8 * d:1288 + 8 * d], H[:, k, :], start=(k == 0), stop=(k == 15))
                return ins
            S.op("pe", wgT + [HT], [cx.psT[b]], mmd)
            S.op("act", [cx.psT[b], dtbT], [e1T], lambda e, b=b, d=d: e.activation(out=e1[:], in_=cx.ps[b][0:8, :], func=AF.Exp, bias=dtb[:, d:d + 1]))
            S.op("act", [e1T], [DTT], lambda e, d=d: e.activation(out=DT[:, d, :], in_=e1[:], func=AF.Ln, bias=1.0))
        S.dma("sp", [ZT], [], lambda e: e.dma_start(out=zs_o[c0:c0 + TS, :].rearrange("(t p) f -> p t f", p=128), in_=Z[:]))
        S.dma("sp", [XT], [xrawT[ti]], lambda e: e.dma_start(out=xraw_o[:, c0:c0 + TS].rearrange("(c p) n -> p c n", p=128), in_=X[:]))
        S.dma("sp", [DTT], [], lambda e: e.dma_start(out=dt_o[:, :, c0:c0 + TS].rearrange("d h n -> h d n"), in_=DT[:]))
    for ti in range(SQ // TS):
        loadH(ti)
        tile1(ti)
    S.flush()
    xw = [cx.sb("xw", [128, 6, TS + 4], F32) for _ in range(2)]
    acc = [cx.sb("acc", [128, TS], F32) for _ in range(2)]
    xo = [cx.sb("xo", [128, 6, TS], F32) for _ in range(2)]

    def tile2(ti):
        c0 = ti * TS
        W, WT = xw[ti % 2]
        O, OT = xo[ti % 2]
        lo = max(c0 - 2, 0)
        hi = min(c0 + TS + 2, SQ)
        if lo != c0 - 2 or hi != c0 + TS + 2:
            S.op("dve", [], [WT], lambda e: e.memset(W[:], 0.0))
        d0 = lo - (c0 - 2)
        S.dma("sp", [], [WT], lambda e: e.dma_start(out=W[:, :, d0:d0 + hi - lo], in_=xraw_o[:, lo:hi].rearrange("(c p) n -> p c n", p=128)))
        for c in range(6):
            A, AT = acc[c % 2]
            S.op("dve", [WT, cwT], [AT], lambda e, c=c, A=A: e.tensor_scalar(A[:], W[:, c, 0:TS], cw[:, c, 0:1], None, op0=ALU.mult))
            for k in range(1, 5):
                S.op("dve", [WT, cwT, AT], [AT], lambda e, c=c, k=k, A=A: e.scalar_tensor_tensor(A[:], W[:, c, k:k + TS], cw[:, c, k:k + 1], A[:], op0=ALU.mult, op1=ALU.add))
            S.op("act", [AT, cbT], [OT], lambda e, c=c, A=A: e.activation(out=O[:, c, :], in_=A[:], func=AF.Silu, bias=cb[:, c:c + 1]))
        S.dma("sp", [OT], [], lambda e: e.dma_start(out=xbc_o[:, c0:c0 + TS].rearrange("(c p) n -> p c n", p=128), in_=O[:]))
    for ti in range(SQ // TS):
        tile2(ti)


def make_scan(cx, SQ, xbcT_ap, xtm_ap, btm_ap, dtT_ap, alog_ap, y_o, fwd, write_x, pb):
    S = cx.S
    SC = 512
    first = True
    bM, bA, bY, bG = pb, pb + 1, pb + 2, pb + 3
    mxT = mdT = mcT = mbT = cx.psT[bM]
    al, alT = load_small(cx, alog_ap, [8, 1], F32, "al")
    a, aT = cx.sb("a", [8, 1], F32)
    S.op("act", [alT], [aT], lambda e: e.activation(out=a[:], in_=al[:], func=AF.Exp))
    S.op("dve", [aT], [aT], lambda e: e.tensor_scalar(a[:], a[:], -1.0, None, op0=ALU.mult))
    ident, identT = cx.sb("ident", [128, 128], F32)
    S.op("dve", [], [identT], lambda e: e.memset(ident[:], 0.0))
    S.op("pool", [identT], [identT], lambda e: e.affine_select(out=ident[:], in_=ident[:], pattern=[[-1, 128]], compare_op=ALU.not_equal, fill=1.0, base=0, channel_multiplier=1))
    maskL, maskLT = cx.sb("maskL", [128, 128], F32)
    S.op("dve", [], [maskLT], lambda e: e.memset(maskL[:], 1.0))
    if fwd:
        S.op("pool", [maskLT], [maskLT], lambda e: e.affine_select(out=maskL[:], in_=maskL[:], pattern=[[1, 128]], compare_op=ALU.is_ge, fill=0.0, base=0, channel_multiplier=-1))
    else:
        S.op("pool", [maskLT], [maskLT], lambda e: e.affine_select(out=maskL[:], in_=maskL[:], pattern=[[-1, 128]], compare_op=ALU.is_ge, fill=0.0, base=0, channel_multiplier=1))
    colL = 127 if fwd else 0
    XTs = [cx.sb("XTs", [128, 4, SC], F32) for _ in range(2)] if first else None
    BTf = [cx.sb("BTf", [128, SC], F32) for _ in range(2)] if first else None
    tmp8, tmp8T = cx.sb("tmp8", [8, 128], F32)
    sel, selT = cx.sb("sel", [8, 8, 128], F32)
    S.op("dve", [], [selT], lambda e: e.memset(sel[:], 0.0))
    S.op("pool", [selT], [selT], lambda e: e.affine_select(out=sel[:], in_=sel[:], pattern=[[-1, 8], [0, 128]], compare_op=ALU.not_equal, fill=1.0, base=0, channel_multiplier=1))
    zc, zcT = cx.sb("zc", [128, 1], F32)
    S.op("dve", [], [zcT], lambda e: e.memset(zc[:], 0.0))
    ones8, ones8T = cx.sb("ones8", [8, 128], F32)
    S.op("dve", [], [ones8T], lambda e: e.memset(ones8[:], 1.0))
    st32, st32T = cx.sb("st32", [128, 512], F32)
    S.op("dve", [], [st32T], lambda e: e.memset(st32[:], 0.0))
    stbf, stbfT = cx.sb("stbf", [128, 512], BF16)
    S.op("dve", [], [stbfT], lambda e: e.memset(stbf[:], 0.0))
    BTs = [cx.sb("BTs", [128, SC], BF16) for _ in range(2)]
    CTs = [cx.sb("CTs", [128, SC], BF16) for _ in range(2)]
    xs = [cx.sb("xs", [128, 4, 512], F32) for _ in range(2)]
    bs = [cx.sb("bs", [128, 4, 128], BF16) for _ in range(2)]
    dts = [cx.sb("dts", [128, 4, 8], F32) for _ in range(2)]
    dtTs = [cx.sb("dtTs", [8, SC], F32) for _ in range(2)]
    yo = [cx.sb("yo", [128, 4, 512], F32) for _ in range(2)]
    dta, dtaT = cx.sb("dta", [8, 128], F32)
    cumT, cumTT = cx.sb("cumT", [8, 128], F32)
    cum, cumtT = cx.sb("cum", [128, 8], F32)
    cbm, cbmT = cx.sb("cbm", [128, 128], F32)
    seg, segT = cx.sb("seg", [128, 8, 128], F32)
    wT, wTT = cx.sb("wT", [128, 8, 128], BF16)
    sm, smT = cx.sb("sm", [128, 5, 8], F32)
    xdt, xdtT = cx.sb("xdt", [128, 512], BF16)
    xe, xeT = cx.sb("xe", [128, 512], BF16)
    gy, gyT = cx.sb("gy", [128, 512], F32)

    def chunk(sc, k, bufs):
        (B_, BT_), (C_, CT_), (X_, XT_), (Bm, BmT), (D_, DT_), (DTr, DTrT), (Y_, YT_) = bufs
        cs = slice(k * 128, (k + 1) * 128)
        S.op("dve", [DTrT, aT], [dtaT], lambda e: e.tensor_scalar(dta[:], DTr[:, cs], a[:, 0:1], None, op0=ALU.mult))
        S.op("dve", [dtaT, ones8T], [cumTT], lambda e: e.tensor_tensor_scan(cumT[:], ones8[:], dta[:], 0.0, op0=ALU.mult, op1=ALU.add))
        if not fwd:
            S.op("dve", [dtaT, cumTT], [tmp8T], lambda e: e.tensor_tensor(tmp8[:], dta[:], cumT[:], op=ALU.subtract))
            S.op("dve", [tmp8T, cumTT], [tmp8T], lambda e: e.tensor_scalar(tmp8[:], tmp8[:], cumT[:, 127:128], None, op0=ALU.add))
            S.op("dve", [tmp8T], [cumTT], lambda e: e.tensor_copy(cumT[:], tmp8[:]))
        pA0, pA1, pY, pG, pS = bA, bG, bY, bG, bA
        S.op("pe", [DTrT, identT], [mdT], lambda e: e.transpose(cx.ps[bM][:, 8:16], DTr[:, cs], ident[0:8, 0:8]))
        S.op("dve", [mdT], [DT_], lambda e: e.tensor_copy(D_[:, k, :], cx.ps[bM][:, 8:16]))
        if first:
            XT4, XT4T = XTs[sc % 2]
            BF_, BFT_ = BTf[sc % 2]

            def trx(e):
                ins = None
                for c in range(4):
                    ins = e.transpose(cx.ps[pY][:, c * 128:(c + 1) * 128], XT4[:, c, cs], ident[:])
                return ins
            S.op("pe", [XT4T, identT], [cx.psT[pY]], trx)
            S.op("act", [cx.psT[pY]], [XT_], lambda e: e.activation(out=X_[:, k, :], in_=cx.ps[pY][:, :], func=AF.Copy))
            S.op("pe", [BFT_, identT], [mbT], lambda e: e.transpose(cx.ps[bM][:, 256:384], BF_[:, cs], ident[:]))
            S.op("act", [mbT], [BmT], lambda e: e.activation(out=Bm[:, k, :], in_=cx.ps[bM][:, 256:384], func=AF.Copy))
        S.op("pe", [cumTT, identT], [mxT], lambda e: e.transpose(cx.ps[bM][:, 0:8], cumT[:, :], ident[0:8, 0:8]))
        S.op("dve", [mxT], [cumtT], lambda e: e.tensor_copy(cum[:], cx.ps[bM][:, 0:8]))

        def mmA(e, half):
            ins = None
            for hh in range(4):
                ins = e.matmul(cx.ps[(pA0, pA1)[half]][:, hh * 128:(hh + 1) * 128], sel[:, half * 4 + hh, :], cumT[:, :], start=True, stop=True)
            return ins
        S.op("pe", [selT, cumTT], [cx.psT[pA0]], lambda e: mmA(e, 0))
        S.op("pe", [selT, cumTT], [cx.psT[pA1]], lambda e: mmA(e, 1))
        pAs = (pA0, pA1)
        S.op("pe", [BT_, CT_], [mcT], lambda e: e.matmul(cx.ps[bM][:, 128:256], B_[:, cs], C_[:, cs], start=True, stop=True))
        S.op("dve", [mcT, maskLT], [cbmT], lambda e: e.tensor_tensor(cbm[:], cx.ps[bM][:, 128:256], maskL[:], op=ALU.mult))
        for half in range(2):
            S.op("dve", [cx.psT[pAs[half]], cumtT, segT], [segT], lambda e, half=half: e.tensor_tensor(
                seg[:, half * 4:half * 4 + 4, :], cx.ps[pAs[half]][:, :].rearrange("p (h i) -> p h i", i=128),
                cum[:, half * 4:half * 4 + 4].unsqueeze(2).to_broadcast([128, 4, 128]), op=ALU.subtract))
        S.op("dve", [segT], [segT], lambda e: e.tensor_scalar(seg[:], seg[:], 0.0, None, op0=ALU.min))
        for half in range(2):
            S.op("dve", [cx.psT[pAs[half]]], [smT], lambda e, half=half: e.tensor_copy(
                sm[:, 0, half * 4:half * 4 + 4], cx.ps[pAs[half]][:, :].rearrange("p (h i) -> p h i", i=128)[:, :, colL]))
        S.op("act", [segT], [segT], lambda e: e.activation(out=seg[:], in_=seg[:], func=AF.Exp))
        S.op("dve", [segT, cbmT], [wTT], lambda e: e.tensor_tensor(wT[:], seg[:], cbm[:].unsqueeze(1).to_broadcast([128, 8, 128]), op=ALU.mult))
        S.op("dve", [smT, cumtT], [smT], lambda e: e.tensor_tensor(sm[:, 1, :], sm[:, 0, :], cum[:], op=ALU.subtract))
        S.op("act", [smT], [smT], lambda e: e.activation(out=sm[:, 1, :], in_=sm[:, 1, :], func=AF.Exp))
        S.op("act", [smT], [smT], lambda e: e.activation(out=sm[:, 2, :], in_=sm[:, 0, :], func=AF.Exp))
        S.op("act", [smT, cumtT], [smT], lambda e: e.activation(out=sm[:, 3, :], in_=cum[:], func=AF.Exp))
        S.op("dve", [smT, DT_], [smT], lambda e: e.tensor_tensor(sm[:, 4, :], sm[:, 1, :], D_[:, k, :], op=ALU.mult))
        hv = lambda ap: ap.rearrange("p (h q) -> p h q", q=64)
        S.op("dve", [XT_, DT_], [xdtT], lambda e: e.tensor_tensor(hv(xdt[:]), hv(X_[:, k, :]), D_[:, k, :].unsqueeze(2).to_broadcast([128, 8, 64]), op=ALU.mult))
        S.op("dve", [XT_, smT], [xeT], lambda e: e.tensor_tensor(hv(xe[:]), hv(X_[:, k, :]), sm[:, 4, :].unsqueeze(2).to_broadcast([128, 8, 64]), op=ALU.mult))

        def mmY(e):
            ins = None
            for h in range(8):
                ins = e.matmul(cx.ps[pY][:, h * 64:(h + 1) * 64], wT[:, h, :], xdt[:, h * 64:(h + 1) * 64], start=True, stop=True)
            return ins
        S.op("pe", [wTT, xdtT], [cx.psT[pY]], mmY)
        S.op("pe", [CT_, stbfT], [cx.psT[pG]], lambda e: e.matmul(cx.ps[pG][:, :], C_[:, cs], stbf[:], start=True, stop=True))
        for h in range(8):
            hs = slice(h * 64, (h + 1) * 64)
            S.op("act", [cx.psT[pG], smT], [gyT], lambda e, h=h, hs=hs: e.activation(out=gy[:, hs], in_=cx.ps[pG][:, hs], func=AF.Copy, scale=sm[:, 3, h:h + 1]))
        S.op("dve", [gyT, cx.psT[pY]], [YT_], lambda e: e.tensor_tensor(Y_[:, k, :], gy[:], cx.ps[pY][:, :], op=ALU.add))
        S.op("pe", [BmT, xeT], [cx.psT[pS]], lambda e: e.matmul(cx.ps[pS][:, :], Bm[:, k, :], xe[:], start=True, stop=True))
        S.op("dve", [st32T, smT], [st32T], lambda e: e.tensor_tensor(hv(st32[:]), hv(st32[:]), sm[:, 2, :].unsqueeze(2).to_broadcast([128, 8, 64]), op=ALU.mult))
        S.op("dve", [st32T, cx.psT[pS]], [st32T], lambda e: e.tensor_tensor(st32[:], st32[:], cx.ps[pS][:, :], op=ALU.add))
        S.op("act", [st32T], [stbfT], lambda e: e.activation(out=stbf[:], in_=st32[:], func=AF.Copy))

    def bufs_of(sc):
        return [BTs[sc % 2], CTs[sc % 2], xs[sc % 2], bs[sc % 2], dts[sc % 2], dtTs[sc % 2], yo[sc % 2]]

    def load(sc):
        c0 = sc * SC
        (B_, BT_), (C_, CT_), (X_, XT_), (Bm, BmT), (D_, DT_), (DTr, DTrT), (Y_, YT_) = bufs_of(sc)
        S.dma("pool", [], [BT_], lambda e: e.dma_start(out=B_[:], in_=xbcT_ap[512:640, c0:c0 + SC]))
        S.dma("pool", [], [CT_], lambda e: e.dma_start(out=C_[:], in_=xbcT_ap[640:768, c0:c0 + SC]))
        XT4, XT4T = XTs[sc % 2]
        BF_, BFT_ = BTf[sc % 2]
        S.dma("sp", [], [XT4T], lambda e: e.dma_start(out=XT4[:], in_=xbcT_ap[0:512, c0:c0 + SC].rearrange("(c p) n -> p c n", p=128)))
        S.dma("sp", [], [BFT_], lambda e: e.dma_start(out=BF_[:], in_=xbcT_ap[512:640, c0:c0 + SC]))
        S.dma("sp", [], [DTrT], lambda e: e.dma_start(out=DTr[:], in_=dtT_ap[:, c0:c0 + SC]))

    def do_chunk(sc, k):
        chunk(sc, k, bufs_of(sc))

    def store(sc):
        c0 = sc * SC
        (B_, BT_), (C_, CT_), (X_, XT_), (Bm, BmT), (D_, DT_), (DTr, DTrT), (Y_, YT_) = bufs_of(sc)
        S.dma("sp", [YT_], [], lambda e: e.dma_start(out=y_o[c0:c0 + SC, :].rearrange("(t p) f -> p t f", p=128), in_=Y_[:]))
        if write_x:
            S.dma("sp", [XT_], [], lambda e: e.dma_start(out=xtm_ap[c0:c0 + SC, :].rearrange("(t p) f -> p t f", p=128), in_=X_[:]))
    return load, do_chunk, store


def phase_A1b2(cx, SQ, xbcT_ap, xtm_ap, btm_ap, dt_ap, alog_f, alog_b, yf_o, yb_o):
    F = make_scan(cx, SQ, xbcT_ap, xtm_ap, btm_ap, dt_ap[0], alog_f, yf_o, True, True, 0)
    B = make_scan(cx, SQ, xbcT_ap, xtm_ap, btm_ap, dt_ap[1], alog_b, yb_o, False, False, 4)
    nsc = SQ // 512
    for s_ in range(nsc):
        sf, sb_ = s_, nsc - 1 - s_
        F[0](sf)
        B[0](sb_)
        for k in range(4):
            F[1](sf, k)
            B[1](sb_, 3 - k)
        F[2](sf)
        B[2](sb_)


def phase_A1c(cx, SQ, yf_ap, yb_ap, xtm_ap, ztm_ap, dbc_ap, ngbc_ap, yg_o):
    S = cx.S
    dbc, dbcT = cx.sb("dbc", [128, 512], F32)
    S.dma("sp", [], [dbcT], lambda e: e.dma_start(out=dbc[:], in_=bc_rows(dbc_ap, 128)))
    ngb, ngbT = cx.sb("ngb", [128, 512], F32)
    S.dma("sp", [], [ngbT], lambda e: e.dma_start(out=ngb[:], in_=bc_rows(ngbc_ap, 128)))
    bufs = [[cx.sb(n_, [128, 4, 512], F32) for n_ in ("yf", "yb", "x", "z")] for _ in range(2)]
    outs = [cx.sb("o", [128, 4, 512], BF16) for _ in range(2)]
    junk, junkT = cx.sb("junk", [128, 512], F32)
    st, stT = cx.sb("st", [128, 4], F32)

    def blk(ti):
        c0 = ti * 512
        (YF, YFT), (YB, YBT), (X, XT), (Z, ZT) = bufs[ti % 2]
        O, OT = outs[ti % 2]
        for (t_, tT_, src) in ((YF, YFT, yf_ap), (YB, YBT, yb_ap), (X, XT, xtm_ap), (Z, ZT, ztm_ap)):
            S.dma("sp", [], [tT_], lambda e, t_=t_, src=src: e.dma_start(out=t_[:], in_=src[c0:c0 + 512, :].rearrange("(t p) f -> p t f", p=128)))
        for k in range(4):
            S.op("dve", [XT, dbcT], [XT], lambda e, k=k: e.tensor_tensor(X[:, k, :], X[:, k, :], dbc[:], op=ALU.mult))
            S.op("dve", [XT, YFT], [XT], lambda e, k=k: e.tensor_tensor(X[:, k, :], X[:, k, :], YF[:, k, :], op=ALU.add))
            S.op("dve", [XT, YBT], [XT], lambda e, k=k: e.tensor_tensor(X[:, k, :], X[:, k, :], YB[:, k, :], op=ALU.add))
            S.op("dve", [XT, ZT], [XT], lambda e, k=k: e.tensor_tensor(X[:, k, :], X[:, k, :], Z[:, k, :], op=ALU.mult))
            S.op("dve", [], [stT], lambda e: e.memset(st[:], 0.0))
            S.op("act", [XT, stT], [junkT, stT], lambda e, k=k: e.activation(out=junk[:], in_=X[:, k, :], func=AF.Square, accum_out=st[:, 0:1]))
            S.op("act", [stT], [stT], lambda e: e.activation(out=st[:, 1:2], in_=st[:, 0:1], func=AF.Sqrt, scale=1.0 / 512, bias=EPS))
            S.op("dve", [stT], [stT], lambda e: e.reciprocal(st[:, 2:3], st[:, 1:2]))
            S.op("dve", [XT, stT, ngbT], [OT], lambda e, k=k: e.scalar_tensor_tensor(O[:, k, :], X[:, k, :], st[:, 2:3], ngb[:], op0=ALU.mult, op1=ALU.mult))
        S.dma("sp", [OT], [], lambda e: e.dma_start(out=yg_o[c0:c0 + 512, :].rearrange("(t p) f -> p t f", p=128), in_=O[:]))
    for ti in range(SQ // 512):
        blk(ti)

def phase_B0(cx, NQ, SK, qT_ap, kT_ap, kpeT_ap, v_ap, bT_o):
    S = cx.S
    NKT = SK // 128
    NQT = NQ // 512
    scale = 192 ** -0.5
    kpe, kpeT = cx.sb("kpe", [64, SK], BF16)
    gathered = len(kT_ap.shape) == 2
    if gathered:
        S.dma("sp", [], [kpeT], lambda e: e.dma_start(out=kpe[:].rearrange("p (r n) -> p r n", r=8), in_=kpeT_ap.rearrange("(r p) n -> p r n", r=8)))
    else:
        S.dma("sp", [], [kpeT], lambda e: e.dma_start(out=kpe[:], in_=kpeT_ap))
    kb = [cx.sb("kb", [128, SK], BF16) for _ in range(2)]
    vb = [cx.sb("vb", [128, NKT, 128], BF16) for _ in range(2)]
    qn = [cx.sb("qn", [128, NQ], BF16) for _ in range(2)]
    qp = [cx.sb("qp", [64, NQ], BF16) for _ in range(2)]
    pt = [cx.sb("pt", [128, 512], BF16) for _ in range(4)]
    rl, rlT = cx.sb("rl", [128, 512], F32)
    accs = [cx.sb("acc", [128, 512], F32) for _ in range(2)]
    ones32, ones32T = cx.sb("ones32", [128, 128], F32)
    S.op("dve", [], [ones32T], lambda e: e.memset(ones32[:], 1.0))
    ob = [cx.sb("ob", [128, 512], BF16) for _ in range(2)]
    def do_hq(h, qt, it, K_, KT_, V_, VT_, Qn, QnT, Qp, QpT):
        qs = slice(qt * 512, (qt + 1) * 512)
        bO, bL = (4, 5) if it % 2 == 0 else (6, 7)
        def sc(kt):
            b = kt % 4
            def mm(e, b=b, kt=kt):
                e.matmul(cx.ps[b][:, :], K_[:, kt * 128:(kt + 1) * 128], Qn[:, qs], start=True, stop=False)
                return e.matmul(cx.ps[b][:, :], kpe[:, kt * 128:(kt + 1) * 128], Qp[:, qs], start=False, stop=True)
            S.op("pe", [KT_, kpeT, QnT, QpT], [cx.psT[b]], mm)
            P_, PT_ = pt[b]
            S.op("act", [cx.psT[b]], [PT_], lambda e, b=b, P_=P_: e.activation(out=P_[:], in_=cx.ps[b][:, :], func=AF.Exp, scale=scale))

        AC, ACT_ = accs[it % 2]

        def pv(kt):
            P_, PT_ = pt[kt % 4]
            S.op("pe", [VT_, PT_], [cx.psT[bO]], lambda e, kt=kt, P_=P_: e.matmul(cx.ps[bO][:, :], V_[:, kt, :], P_[:], start=(kt == 0), stop=(kt == NKT - 1)))
            if kt == 0:
                S.op("dve", [PT_], [ACT_], lambda e, P_=P_: e.tensor_copy(AC[:], P_[:]))
            else:
                S.op("dve", [PT_, ACT_], [ACT_], lambda e, P_=P_: e.tensor_tensor(AC[:], AC[:], P_[:], op=ALU.add))
        sc(0)
        sc(1)
        for kt in range(NKT):
            if kt + 2 < NKT:
                sc(kt + 2)
            pv(kt)
        O_, OT_ = ob[it % 2]
        S.op("pe", [ACT_, ones32T], [cx.psT[bL]], lambda e: e.matmul(cx.ps[bL][:, :], ones32[:, :], AC[:], start=True, stop=True))
        S.op("dve", [cx.psT[bL]], [rlT], lambda e, bL=bL: e.reciprocal(rl[:], cx.ps[bL][:, :]))
        S.op("dve", [cx.psT[bO], rlT], [OT_], lambda e, bO=bO, O_=O_: e.tensor_tensor(O_[:], cx.ps[bO][:, :], rl[:], op=ALU.mult))
        S.dma("sp", [OT_], [], lambda e, O_=O_, h=h, qs=qs: e.dma_start(out=bT_o[h * 128:(h + 1) * 128, qs], in_=O_[:]))

    it = 0
    for h in range(8):
        K_, KT_ = kb[h % 2]
        V_, VT_ = vb[h % 2]
        Qn, QnT = qn[h % 2]
        Qp, QpT = qp[h % 2]
        if gathered:
            S.dma("sp", [], [KT_], lambda e, K_=K_, h=h: e.dma_start(out=K_[:].rearrange("p (r n) -> p r n", r=8), in_=kT_ap.rearrange("(r h p) n -> h p r n", r=8, h=8)[h]))
        else:
            S.dma("sp", [], [KT_], lambda e, K_=K_, h=h: e.dma_start(out=K_[:], in_=kT_ap[h]))
        S.dma("sp", [], [VT_], lambda e, V_=V_, h=h: e.dma_start(out=V_[:], in_=v_ap[:, h * 128:(h + 1) * 128].rearrange("(t p) d -> p t d", p=128)))
        S.dma("sp", [], [QnT], lambda e, Qn=Qn, h=h: e.dma_start(out=Qn[:], in_=qT_ap[h, 0:128, :]))
        S.dma("sp", [], [QpT], lambda e, Qp=Qp, h=h: e.dma_start(out=Qp[:], in_=qT_ap[h, 128:192, :]))
        for qt in range(NQT):
            do_hq(h, qt, it, K_, KT_, V_, VT_, Qn, QnT, Qp, QpT)
            it += 1


def phase_MOD(cx, cT_ap, adaw_ap, adab_ap, mod_o):
    S = cx.S
    c, cT = load_small(cx, cT_ap, [128, 16], F32, "c")
    cond, condT = cx.sb("cond", [128, 16], F32)
    S.op("act", [cT], [condT], lambda e: e.activation(out=cond[:], in_=c[:], func=AF.Silu))
    wb = [cx.sb("aw", [128, 16, 512], F32) for _ in range(2)]
    ab, abT = load_small(cx, adab_ap.rearrange("l p j -> p l j"), [128, 2, 12], F32, "ab")
    mo, moT = cx.sb("mo", [128, 2, 12], F32)
    i = 0
    for l in range(2):
        for blk in range(3):
            W, WT = wb[i % 2]
            i += 1
            S.dma("sp", [], [WT], lambda e, W=W, l=l, blk=blk: e.dma_start(out=W[:], in_=adaw_ap[l, :, blk * 512:(blk + 1) * 512].rearrange("(c p) m -> p c m", p=128)))
            def mm(e, W=W, l=l, blk=blk):
                ins = None
                for m in range(4):
                    j = l * 12 + blk * 4 + m
                    for k in range(16):
                        ins = e.matmul(cx.ps[0][:, j:j + 1], W[:, k, m * 128:(m + 1) * 128], cond[:, k:k + 1], start=(k == 0), stop=(k == 15))
                return ins
            S.op("pe", [WT, condT], [cx.psT[0]], mm)
    S.op("dve", [cx.psT[0], abT], [moT], lambda e: e.tensor_tensor(mo[:].rearrange("p l j -> p (l j)"), cx.ps[0][:, 0:24], ab[:].rearrange("p l j -> p (l j)"), op=ALU.add))
    S.dma("sp", [moT], [], lambda e: e.dma_start(out=mod_o.rearrange("l p j -> p l j"), in_=mo[:]))
```

```python
class T:
    __slots__ = ("name", "w", "r")

    def __init__(self, name=""):
        self.name = name
        self.w = None
        self.r = []


class _Op:
    __slots__ = ("eng", "dma", "fn", "deps", "sig", "sem", "val", "idx", "bar")

    def __init__(self, eng, dma, fn):
        self.eng = eng
        self.dma = dma
        self.fn = fn
        self.deps = []
        self.sig = False
        self.sem = None
        self.val = 0
        self.bar = None


ENGS = ("pe", "act", "dve", "pool", "sp")


class Sched:
    NDMA = {"sp": 6, "pool": 6, "act": 2}

    NCC = 4

    def __init__(self, nc, es):
        self.nc = nc
        self.es = es
        self.ops = []
        self.tiles = []
        self.csem = {e: es.enter_context(nc.semaphore(f"c_{e}")) for e in ("pe", "act", "dve", "pool")}
        self.dsem = {e: [es.enter_context(nc.semaphore(f"d_{e}{i}")) for i in range(n)] for e, n in self.NDMA.items()}
        self.ccsem = [es.enter_context(nc.semaphore(f"cc{i}")) for i in range(self.NCC)]
        self.ccount = {e: 0 for e in self.csem}
        self.dcount = {e: [0] * n for e, n in self.NDMA.items()}
        self.cccount = [0] * self.NCC
        self.drr = {e: 0 for e in self.NDMA}
        self.ccrr = 0
        self.last_compute = {}

    def tile(self, name=""):
        t = T(name)
        self.tiles.append(t)
        return t

    def tiles_n(self, n, name=""):
        return [self.tile(f"{name}{i}") for i in range(n)]

    def _add(self, eng, dma, reads, writes, fn):
        op = _Op(eng, dma, fn)
        deps = set()
        for t in list(reads) + list(writes):
            if t.w is not None:
                deps.add(t.w)
        for t in writes:
            for r in t.r:
                deps.add(r)
        for d in deps:
            if d is op:
                continue
            if (not dma) and (not d.dma) and d.eng == eng:
                if eng == "pe":
                    continue
                if not any((t.w is d) for t in reads):
                    continue
            op.deps.append(d)
        for t in reads:
            t.r.append(op)
        for t in writes:
            t.w = op
            t.r = []
        self.ops.append(op)
        return op

    def op(self, eng, reads, writes, fn):
        return self._add(eng, False, reads, writes, fn)

    def dma(self, eng, reads, writes, fn):
        return self._add(eng, True, reads, writes, fn)

    def cc(self, reads, writes, fn):
        op = self._add("pool", True, reads, writes, fn)
        op.bar = "cc"
        return op

    def barrier(self):
        op = _Op("bar", False, None)
        self.ops.append(op)
        for t in self.tiles:
            t.w = None
            t.r = []
        return op

    def finish(self):
        self.flush()

    def flush(self):
        nc = self.nc
        self.barrier()
        ops = self.ops
        self.ops = []
        csem, dsem, ccsem = self.csem, self.dsem, self.ccsem
        for op in ops:
            if op.eng == "bar":
                for o in self.last_compute.values():
                    o.sig = True
                continue
            for d in op.deps:
                d.sig = True
            if not op.dma:
                self.last_compute[op.eng] = op
        for op in ops:
            if op.eng == "bar":
                op.bar = (dict(self.last_compute_snapshot(op, ops)), {k: list(v) for k, v in self.dcount.items()}, list(self.cccount))
                continue
            if op.dma and op.bar == "cc":
                i = self.ccrr
                self.ccrr = (i + 1) % self.NCC
                self.cccount[i] += 1
                op.sem = ccsem[i]
                op.val = self.cccount[i]
            elif op.dma:
                i = self.drr[op.eng]
                self.drr[op.eng] = (i + 1) % self.NDMA[op.eng]
                self.dcount[op.eng][i] += 16
                op.sem = dsem[op.eng][i]
                op.val = self.dcount[op.eng][i]
            elif op.sig and op.sem is None:
                self.ccount[op.eng] += 1
                op.sem = csem[op.eng]
                op.val = self.ccount[op.eng]
        per = {e: [] for e in ENGS}
        for op in ops:
            if op.eng == "bar":
                for e in ENGS:
                    per[e].append(op)
            else:
                per[op.eng].append(op)
        with nc.Block() as blk:
            def run(ename, eng):
                seen = {}

                def wait(sem, val):
                    if val <= 0:
                        return
                    k = id(sem)
                    if seen.get(k, 0) >= val:
                        return
                    seen[k] = val
                    eng.wait_ge(sem, val)

                for op in per[ename]:
                    if op.eng == "bar":
                        lc, dc, cc = op.bar
                        for o in lc.values():
                            wait(o.sem, o.val)
                        for e, vals in dc.items():
                            for i, v in enumerate(vals):
                                wait(dsem[e][i], v)
                        for i, v in enumerate(cc):
                            wait(ccsem[i], v)
                        continue
                    for d in op.deps:
                        wait(d.sem, d.val)
                    if op.dma and op.bar == "cc":
                        if op.val > 1:
                            wait(op.sem, op.val - 1)
                        op.fn(eng).then_inc(op.sem)
                        continue
                    if op.dma and op.val > 16:
                        wait(op.sem, op.val - 16)
                    ins = op.fn(eng)
                    if op.dma:
                        ins.then_inc(op.sem, 16)
                    elif op.sig:
                        ins.then_inc(op.sem, 1)

            @blk.tensor
            def _(e):
                run("pe", e)

            @blk.scalar
            def _(e):
                run("act", e)

            @blk.vector
            def _(e):
                run("dve", e)

            @blk.gpsimd
            def _(e):
                run("pool", e)

            @blk.sync
            def _(e):
                run("sp", e)

    def last_compute_snapshot(self, bar_op, ops):
        snap = dict(getattr(self, "_lc_prev", {}))
        for op in ops:
            if op is bar_op:
                break
            if op.eng != "bar" and not op.dma:
                snap[op.eng] = op
        if bar_op is ops[-1]:
            self._lc_prev = dict(snap)
        return snap

import numpy as np
from contextlib import ExitStack
import concourse.bass as bass
import concourse.mybir as mybir
from concourse.bass_utils import run_bass_kernel_spmd

F32 = mybir.dt.float32
BF16 = mybir.dt.bfloat16
I32 = mybir.dt.int32
AF = mybir.ActivationFunctionType
ALU = mybir.AluOpType
AX = mybir.AxisListType
EPS = 1e-6
NCORES = 8


class Cx:
    def __init__(self, name="k"):
        self.nc = bass.Bass("TRN2", target_bir_lowering=False)
        self.es = ExitStack()
        self.S = Sched(self.nc, self.es)
        self.nps = 0
        self.ps = []
        self.psT = []
        for i in range(8):
            p = self.es.enter_context(self.nc.psum_tensor(f"ps{i}", [128, 512], F32))
            self.ps.append(p)
            self.psT.append(self.S.tile(f"ps{i}"))
        self.uid = 0
        self.pes = None
        self.ones_bf, self.ones_bfT = self.sb("ones_bf", [128, 128], BF16)
        self.S.op("dve", [], [self.ones_bfT], lambda e: e.memset(self.ones_bf[:], 1.0))
        self._keep = set(self.S.tiles)

    def din(self, name, shape, dt=F32):
        return self.nc.dram_tensor(name, list(shape), dt, kind="ExternalInput").ap()

    def dout(self, name, shape, dt=F32):
        return self.nc.dram_tensor(name, list(shape), dt, kind="ExternalOutput").ap()

    def sb(self, name, shape, dt, es=None):
        self.uid += 1
        t = (es or getattr(self, "pes", None) or self.es).enter_context(self.nc.sbuf_tensor(f"{name}_{self.uid}", list(shape), dt))
        return t, self.S.tile(name)

    def dint(self, name, shape, dt=F32, shared=False):
        if shared:
            return self.nc.dram_tensor(name, list(shape), dt, addr_space="Shared").ap()
        return self.nc.dram_tensor(name, list(shape), dt).ap()

    def phase_begin(self):
        self.pes = ExitStack()

    def phase_end(self):
        self.S.flush()
        self.pes.close()
        self.pes = None
        self.S.tiles = [t for t in self.S.tiles if t in self._keep]

    def next_ps(self, n=1):
        if self.nps + n > 6:
            self.nps = 0
        r = list(range(self.nps, self.nps + n))
        self.nps = (self.nps + n) % 6
        return r

    def finish(self):
        self.S.finish()
        self.es.close()
        return self.nc


def load_small(cx, dram_ap, shape, dt=F32, name="c", eng="sp"):
    t, tT = cx.sb(name, shape, dt)
    cx.S.dma(eng, [], [tT], lambda e: e.dma_start(out=t[:], in_=dram_ap))
    return t, tT


def gemm_T(cx, w_ap, K, cols, rhs_fn, nt, evac_fn, wbufs, MB=256):
    KC = K // 128
    S = cx.S
    MB = min(MB, (4096 // KC) // 128 * 128) if KC * 128 <= 4096 else 128
    groups = []
    cur = []
    for ci, (c0, n) in enumerate(cols):
        if cur and (c0 == cur[-1][1] + cur[-1][2]) and (c0 + n - cur[0][1] <= MB):
            cur.append((ci, c0, n))
        else:
            if cur:
                groups.append(cur)
            cur = [(ci, c0, n)]
    if cur:
        groups.append(cur)
    if not hasattr(cx, "_wrr"):
        cx._wrr = 0
    for g in groups:
        g0 = g[0][1]
        gw = g[-1][1] + g[-1][2] - g0
        wb, wbT = wbufs[cx._wrr % len(wbufs)]
        cx._wrr += 1
        src = w_ap[:, g0:g0 + gw].rearrange("(c p) m -> p c m", p=128)
        wb = wb[:, 0:KC * gw].rearrange("p (c m) -> p c m", m=gw)
        S.dma("pool", [], [wbT], lambda e, wb=wb, src=src: e.dma_start(out=wb, in_=src))
        for (ci, c0, n) in g:
            off = c0 - g0
            for t in range(nt):
                (b,) = cx.next_ps(1)
                ps = cx.ps[b]
                rr = [rhs_fn(k, t) for k in range(KC)]

                def mm(e, ps=ps, wb=wb, off=off, n=n, rr=rr):
                    ins = None
                    for k in range(KC):
                        ins = e.matmul(ps[0:n, :], wb[:, k, off:off + n], rr[k][0], start=(k == 0), stop=(k == KC - 1))
                    return ins
                S.op("pe", [wbT] + list({id(r[1]): r[1] for r in rr}.values()), [cx.psT[b]], mm)
                evac_fn(ci, t, ps[0:n, :], cx.psT[b])


def sumsq_bcast(cx, sq_fn, nchunks, ps_b):
    rr = [sq_fn(c) for c in range(nchunks)]
    ps = cx.ps[ps_b]

    def mm(e):
        ins = None
        for c in range(nchunks):
            ins = e.matmul(ps[:, :], cx.ones_bf[:, :], rr[c][0], start=(c == 0), stop=(c == nchunks - 1))
        return ins
    cx.S.op("pe", [cx.ones_bfT] + list({id(r[1]): r[1] for r in rr}.values()), [cx.psT[ps_b]], mm)


def rstd_from_ps(cx, ps_b, out_ap, outT, D):
    ps = cx.ps[ps_b]
    cx.S.op("act", [cx.psT[ps_b]], [outT], lambda e: e.activation(out=out_ap, in_=ps[:, :], func=AF.Sqrt, scale=1.0 / D, bias=EPS))
    cx.S.op("dve", [outT], [outT], lambda e: e.reciprocal(out_ap, out_ap))

def mod_cols(cx, mod_ap, name="mod"):
    if len(mod_ap.shape) == 3:
        t, tT = cx.sb(name, [128, 96], F32)
        cx.S.dma("sp", [], [tT], lambda e: e.dma_start(out=t[:].rearrange("p (r j) -> p r j", j=12), in_=mod_ap))
        return t, tT
    return load_small(cx, mod_ap, [128, 96], F32, name)


def norm_T(cx, x_fn, nchunks, ntok, D, sqbufs, rstd, rstdT):
    S = cx.S
    nt = ntok // 512
    banks = [6, 7][:nt]
    for c in range(nchunks):
        xa, xT = x_fn(c)
        sq, sqT = sqbufs[c % len(sqbufs)]
        S.op("act", [xT], [sqT], lambda e, sq=sq, xa=xa: e.activation(out=sq[:, 0:ntok], in_=xa, func=AF.Square))
        for t in range(nt):
            ps = cx.ps[banks[t]]
            S.op("pe", [cx.ones_bfT, sqT], [cx.psT[banks[t]]],
                 lambda e, ps=ps, sq=sq, t=t, c=c: e.matmul(ps[:, :], cx.ones_bf[:, :], sq[:, t * 512:(t + 1) * 512], start=(c == 0), stop=(c == nchunks - 1)))
    for t in range(nt):
        rstd_from_ps(cx, banks[t], rstd[:, t * 512:(t + 1) * 512], rstdT, D)


def phase_C(cx, NT, KM, xT_ap, mixT_ap, w_out_ap, mod_ap, nfg_ap, w1_ap, w2_ap, mode, xo_ap, nmod_ap=None, ng_ap=None, ho_ap=None, gather=None):
    S = cx.S
    D = 2048
    TS = 1024
    KMC = KM // 128
    mod, modT = mod_cols(cx, mod_ap)
    nfg, nfgT = load_small(cx, nfg_ap, [128, 16], F32, "nfg")
    gsc, gscT = cx.sb("gsc", [128, 16], F32)
    S.op("dve", [modT, nfgT], [gscT], lambda e: e.scalar_tensor_tensor(gsc[:], mod[:, 64:80], 1.0, nfg[:], op0=ALU.add, op1=ALU.mult))
    if mode == "mid":
        nmod, nmodT = mod_cols(cx, nmod_ap, "nmod")
        ng, ngT = load_small(cx, ng_ap, [128, 16], F32, "ng")
        gsn, gsnT = cx.sb("gsn", [128, 16], F32)
        S.op("dve", [nmodT, ngT], [gsnT], lambda e: e.scalar_tensor_tensor(gsn[:], nmod[:, 16:32], 1.0, ng[:], op0=ALU.add, op1=ALU.mult))
    else:
        ng, ngT = load_small(cx, ng_ap, [128, 16], F32, "fg")
    x1, _ = cx.sb("x1", [128, 16, TS], F32)
    x1T = cx.S.tiles_n(16, "x1")
    h2, _ = cx.sb("h2", [128, 16, TS], BF16)
    h2T = cx.S.tiles_n(16, "h2")
    shr, _ = cx.sb("shr", [128, 32, TS], BF16)
    shrT = cx.S.tiles_n(32, "shr")
    wbufs = [cx.sb("wb", [128, 4096], BF16) for _ in range(2 if gather is not None else 3)]
    sqbufs = [cx.sb("sq", [128, TS], BF16) for _ in range(2)]
    rstd, rstdT = cx.sb("rstd", [128, TS], F32)
    tmps = [cx.sb("tmp", [128, 512], F32) for _ in range(3)]
    trr = [0]

    def tmp():
        trr[0] += 1
        return tmps[trr[0] % 3]

    def do_st(st):
        c0 = st * TS
        for j in range(4):
            S.dma("sp", [], x1T[4 * j:4 * j + 4], lambda e, j=j: e.dma_start(
                out=x1[:, 4 * j:4 * j + 4, :], in_=xT_ap[512 * j:512 * (j + 1), c0:c0 + TS].rearrange("(c p) n -> p c n", p=128)))
        if gather is None:
            for j in range(KMC // 8):
                src = mixT_ap[j] if isinstance(mixT_ap, (list, tuple)) else mixT_ap[1024 * j:1024 * (j + 1), :]
                S.dma("sp", [], shrT[8 * j:8 * j + 8], lambda e, j=j, src=src: e.dma_start(
                    out=shr[:, 8 * j:8 * j + 8, :], in_=src[:, c0:c0 + TS].rearrange("(c p) n -> p c n", p=128)))
        else:
            yg_ap, ixt, ixT, ident, identT, ytm = gather
            for tb in range(TS // 128):
                col0 = (st * (TS // 128) + tb) * 8
                for g in range(8):
                    Y_, YT_ = ytm[(tb * 8 + g) % len(ytm)]
                    S.dma("pool", [ixT], [YT_], lambda e, Y_=Y_, col=col0 + g: e.indirect_dma_start(
                        out=Y_[:], out_offset=None, in_=yg_ap, in_offset=bass.IndirectOffsetOnAxis(ap=ixt[:, col:col + 1], axis=0)))
                    (b,) = cx.next_ps(1)

                    def tr(e, b=b, Y_=Y_):
                        ins = None
                        for c in range(4):
                            ins = e.transpose(cx.ps[b][:, c * 128:(c + 1) * 128], Y_[:, c * 128:(c + 1) * 128], ident[:])
                        return ins
                    S.op("pe", [YT_, identT], [cx.psT[b]], tr)
                    eng = "act" if (g % 2 == 0) else "dve"
                    if eng == "act":
                        S.op("act", [cx.psT[b]], shrT[4 * g:4 * g + 4], lambda e, b=b, g=g, tb=tb: e.activation(
                            out=shr[:, 4 * g:4 * g + 4, tb * 128:(tb + 1) * 128], in_=cx.ps[b][:, :].rearrange("p (c n) -> p c n", n=128), func=AF.Copy))
                    else:
                        S.op("dve", [cx.psT[b]], shrT[4 * g:4 * g + 4], lambda e, b=b, g=g, tb=tb: e.tensor_copy(
                            shr[:, 4 * g:4 * g + 4, tb * 128:(tb + 1) * 128], cx.ps[b][:, :].rearrange("p (c n) -> p c n", n=128)))
        def ev1(ci, t, ps, psT, gcol=32):
            sl = slice(t * 512, (t + 1) * 512)
            S.op("dve", [psT, x1T[ci], modT], [x1T[ci]], lambda e: e.scalar_tensor_tensor(
                x1[:, ci, sl], ps, mod[:, gcol + ci:gcol + ci + 1], x1[:, ci, sl], op0=ALU.mult, op1=ALU.add))
        gemm_T(cx, w_out_ap, KM, [(m * 128, 128) for m in range(16)],
               lambda k, t: (shr[:, k, t * 512:(t + 1) * 512], shrT[k]), 2, ev1, wbufs)
        norm_T(cx, lambda c: (x1[:, c, :], x1T[c]), 16, TS, D, sqbufs, rstd, rstdT)
        for c in range(16):
            for t in range(2):
                sl = slice(t * 512, (t + 1) * 512)
                tp, tpT = tmp()
                S.op("dve", [x1T[c], rstdT, gscT], [tpT], lambda e, c=c, sl=sl, tp=tp: e.scalar_tensor_tensor(
                    tp[:], x1[:, c, sl], gsc[:, c:c + 1], rstd[:, sl], op0=ALU.mult, op1=ALU.mult))
                S.op("act", [tpT, modT], [h2T[c]], lambda e, c=c, sl=sl, tp=tp: e.activation(
                    out=h2[:, c, sl], in_=tp[:], func=AF.Identity, bias=mod[:, 48 + c:49 + c]))
        for hh in range(2):
            def ev2(ci, t, ps, psT):
                sl = slice(t * 512, (t + 1) * 512)
                tp, tpT = tmp()
                S.op("act", [psT], [tpT], lambda e: e.activation(out=tp[:], in_=ps, func=AF.Relu))
                S.op("dve", [tpT], [shrT[ci]], lambda e: e.tensor_tensor(shr[:, ci, sl], tp[:], tp[:], op=ALU.mult))
            gemm_T(cx, w1_ap, D, [(hh * 4096 + m * 128, 128) for m in range(32)],
                   lambda k, t: (h2[:, k, t * 512:(t + 1) * 512], h2T[k]), 2, ev2, wbufs)
            gemm_T(cx, w2_ap[hh * 4096:(hh + 1) * 4096, :], 4096, [(m * 128, 128) for m in range(16)],
                   lambda k, t: (shr[:, k, t * 512:(t + 1) * 512], shrT[k]), 2,
                   lambda ci, t, ps, psT: ev1(ci, t, ps, psT, gcol=80), wbufs)
        if mode == "mid":
            for j in range(4):
                S.dma("sp", x1T[4 * j:4 * j + 4], [], lambda e, j=j: e.dma_start(
                    out=xo_ap[512 * j:512 * (j + 1), c0:c0 + TS].rearrange("(c p) n -> p c n", p=128), in_=x1[:, 4 * j:4 * j + 4, :]))
        norm_T(cx, lambda c: (x1[:, c, :], x1T[c]), 16, TS, D, sqbufs, rstd, rstdT)
        for c in range(16):
            for t in range(2):
                sl = slice(t * 512, (t + 1) * 512)
                if mode == "mid":
                    tp, tpT = tmp()
                    S.op("dve", [x1T[c], rstdT, gsnT], [tpT], lambda e, c=c, sl=sl, tp=tp: e.scalar_tensor_tensor(
                        tp[:], x1[:, c, sl], gsn[:, c:c + 1], rstd[:, sl], op0=ALU.mult, op1=ALU.mult))
                    S.op("act", [tpT, nmodT], [h2T[c]], lambda e, c=c, sl=sl, tp=tp: e.activation(
                        out=h2[:, c, sl], in_=tp[:], func=AF.Identity, bias=nmod[:, c:c + 1]))
                else:
                    S.op("dve", [x1T[c], rstdT, ngT], [x1T[c]], lambda e, c=c, sl=sl: e.scalar_tensor_tensor(
                        x1[:, c, sl], x1[:, c, sl], ng[:, c:c + 1], rstd[:, sl], op0=ALU.mult, op1=ALU.mult))
        for j in range(4):
            if mode == "mid":
                S.dma("sp", h2T[4 * j:4 * j + 4], [], lambda e, j=j: e.dma_start(
                    out=ho_ap[512 * j:512 * (j + 1), c0:c0 + TS].rearrange("(c p) n -> p c n", p=128), in_=h2[:, 4 * j:4 * j + 4, :]))
            else:
                S.dma("sp", x1T[4 * j:4 * j + 4], [], lambda e, j=j: e.dma_start(
                    out=xo_ap[512 * j:512 * (j + 1), c0:c0 + TS].rearrange("(c p) n -> p c n", p=128), in_=x1[:, 4 * j:4 * j + 4, :]))

    for st_ in range(NT // TS):
        do_st(st_)

import ml_dtypes
_BF = ml_dtypes.bfloat16
_S, _D, _NT = 16384, 2048, 2048


def _lay(v):
    return np.ascontiguousarray(np.asarray(v, np.float32).reshape(-1, 128).T)


def _ca(a):
    return np.ascontiguousarray(a)


def _allgather(cx, src2d, dst2d):
    cx.phase_begin()
    rg = [list(range(NCORES))]
    cx.S.cc([], [], lambda e: e.collective_compute("AllGather", ALU.bypass, replica_groups=rg, ins=[src2d.opt()], outs=[dst2d.opt()]))
    cx.phase_end()


def build_program():
    S_, NT = _S, _NT
    U32 = mybir.dt.uint32
    cx = Cx()
    cx.S.flush()
    d = cx.din
    xT = d("xT", [2048, NT]); cT = d("cT", [128, 16]); adaw = d("adaw", [2, 2048, 1536]); adab = d("adab", [2, 128, 12])
    nmg0 = d("nmg0", [128, 16]); nmg1 = d("nmg1", [128, 16]); nfg0 = d("nfg0", [128, 16]); nfg1 = d("nfg1", [128, 16]); fng = d("fng", [128, 16])
    w_in0 = d("w_in0", [2048, 3136]); sgug = d("sgug", [1, 1024]); wsT = d("wsT", [128, 8, 128]); bs = d("bs", [1, 1024])
    qg = d("qg", [128, 4]); kvg = d("kvg", [128, 4]); w_uq = d("w_uq", [512, 1536]); w_ukv = d("w_ukv", [512, 2048])
    pos = d("pos", [1, NT], I32); invf = d("invf", [64, 1])
    w_out0 = d("w_out0", [2048, 2048]); w1_0 = d("w1_0", [2048, 8192]); w2_0 = d("w2_0", [8192, 2048])
    w1_1 = d("w1_1", [2048, 8192]); w2_1 = d("w2_1", [8192, 2048]); w_out1 = d("w_out1", [4096, 2048])
    wg = d("wg", [2048, 1296]); cw = d("cw", [128, 6, 5]); cb = d("cb", [128, 6]); dtb = d("dtb", [8, 2])
    alog_f = d("alog_f", [8, 1]); alog_b = d("alog_b", [8, 1]); dbc = d("dbc", [1, 512]); ngbc = d("ngbc", [1, 512])
    ix = d("ix", [128, 128], U32)
    outT = cx.dout("outT", [2048, NT])
    di = cx.dint
    mod_loc = di("mod_loc", [2, 128, 12]); mod_g = di("mod_g", [2048, 12])
    cx.phase_begin()
    phase_MOD(cx, cT, adaw, adab, mod_loc)
    cx.phase_end()
    _allgather(cx, mod_loc.rearrange("l p j -> (l p) j"), mod_g)
    modv = lambda l: mod_g.rearrange("(r l p) j -> l p r j", r=8, l=2)[l]
    aT = di("aT", [1024, NT], BF16); qT = di("qT", [8, 192, NT], BF16); kT = di("kT", [8, 128, NT], BF16)
    kpeT = di("kpeT", [64, NT], BF16); v = di("v", [NT, 1024], BF16)
    cx.phase_begin()
    phase_A0(cx, NT, xT, modv(0), nmg0, w_in0, sgug, wsT, bs, qg, kvg, w_uq, w_ukv, pos, invf, aT, qT, kT, kpeT, v)
    cx.phase_end()
    kT_g = di("kT_g", [8 * 1024, NT], BF16, shared=True); kpeT_g = di("kpeT_g", [8 * 64, NT], BF16); v_g = di("v_g", [S_, 1024], BF16, shared=True)
    _allgather(cx, kT.rearrange("h p n -> (h p) n"), kT_g)
    _allgather(cx, kpeT, kpeT_g)
    _allgather(cx, v, v_g)
    bT = di("bT", [1024, NT], BF16)
    cx.phase_begin()
    phase_B0(cx, NT, S_, qT, kT_g, kpeT_g, v_g, bT)
    cx.phase_end()
    x1T = di("x1T", [2048, NT]); ho = di("ho", [2048, NT], BF16)
    cx.phase_begin()
    phase_C(cx, NT, 2048, xT, [aT, bT], w_out0, modv(0), nfg0, w1_0, w2_0, "mid", x1T, nmod_ap=modv(1), ng_ap=nmg1, ho_ap=ho)
    cx.phase_end()
    hT_g = di("hT_g", [8 * 2048, NT], BF16, shared=True)
    _allgather(cx, ho, hT_g)
    ztm = di("ztm", [S_, 512]); xraw = di("xraw", [768, S_]); xbc = di("xbc", [768, S_]); dt = di("dt", [2, 8, S_])
    cx.phase_begin()
    phase_A1a(cx, S_, lambda ti: hT_g[(ti // 4) * 2048:(ti // 4 + 1) * 2048, (ti % 4) * 512:(ti % 4 + 1) * 512], wg, cw, cb, dtb, ztm, xraw, xbc, dt)
    cx.phase_end()
    xtm = di("xtm", [S_, 512]); btm = di("btm", [S_, 128], BF16); yf = di("yf", [S_, 512]); yb = di("yb", [S_, 512])
    cx.phase_begin()
    phase_A1b2(cx, S_, xbc, xtm, btm, dt, alog_f, alog_b, yf, yb)
    cx.phase_end()
    yg = di("yg", [S_, 512], BF16); yg_g = di("yg_g", [8 * S_, 512], BF16, shared=True)
    cx.phase_begin()
    phase_A1c(cx, S_, yf, yb, xtm, ztm, dbc, ngbc, yg)
    cx.phase_end()
    _allgather(cx, yg, yg_g)
    cx.phase_begin()
    ixt, ixT = load_small(cx, ix, [128, 128], U32, "ix")
    ident, identT = cx.sb("identc", [128, 128], F32)
    cx.S.op("dve", [], [identT], lambda e: e.memset(ident[:], 0.0))
    cx.S.op("pool", [identT], [identT], lambda e: e.affine_select(out=ident[:], in_=ident[:], pattern=[[-1, 128]], compare_op=ALU.not_equal, fill=1.0, base=0, channel_multiplier=1))
    ytm = [cx.sb("ytm", [128, 512], F32) for _ in range(2)]
    phase_C(cx, NT, 4096, x1T, None, w_out1, modv(1), nfg1, w1_1, w2_1, "final", outT, ng_ap=fng, gather=(yg_g, ixt, ixT, ident, identT, ytm))
    cx.phase_end()
    cx.es.close()
    return cx.nc


def make_inputs(x, c, positions, ada_w, ada_b, norm_mix_g, norm_ffn_g, ffn_w1, ffn_w2,
                hyb_w_in, sgu_norm_g, sgu_w, sgu_b, mla_q_norm_g, mla_kv_norm_g, mla_w_uq,
                mla_w_ukv, hyb_w_out, ssm_w_in, ssm_conv_w, ssm_conv_b, ssm_dt_bias,
                ssm_a_log, ssm_d, ssm_norm_g, ssm_w_out, final_norm_g):
    f32 = np.float32
    A = lambda a: np.asarray(a, f32)
    x = A(x); c = A(c); positions = np.asarray(positions, np.int32); ada_w = A(ada_w); ada_b = A(ada_b)
    S_, NT, NC = _S, _NT, NCORES
    half = 32
    inv_freq = (10000.0 ** (-np.arange(half, dtype=np.float32) / half)).astype(f32)
    w_in1 = A(ssm_w_in[0]); cw_ = A(ssm_conv_w[0]); cb_ = A(ssm_conv_b[0]); dtb_ = A(ssm_dt_bias[0]); alog_ = A(ssm_a_log[0])
    dsk = A(ssm_d[0]); ngm = A(ssm_norm_g[0])
    com = {"cT": _lay(c[0]), "nmg0": _lay(norm_mix_g[0]), "nmg1": _lay(norm_mix_g[1]), "nfg0": _lay(norm_ffn_g[0]), "nfg1": _lay(norm_ffn_g[1]),
           "fng": _lay(final_norm_g), "w_in0": _ca(A(hyb_w_in[0])), "sgug": _ca(A(sgu_norm_g[0])[None]),
           "wsT": _ca(A(sgu_w[0]).transpose(2, 0, 1)), "bs": _ca(A(sgu_b[0]).reshape(1, -1)), "qg": _lay(mla_q_norm_g[0]), "kvg": _lay(mla_kv_norm_g[0]),
           "w_uq": _ca(A(mla_w_uq[0])), "w_ukv": _ca(A(mla_w_ukv[0])), "invf": _ca(np.concatenate([inv_freq, inv_freq])[:, None].astype(f32)),
           "w_out0": _ca(A(hyb_w_out[0])), "w1_0": _ca(A(ffn_w1[0])), "w2_0": _ca(A(ffn_w2[0])), "w1_1": _ca(A(ffn_w1[1])), "w2_1": _ca(A(ffn_w2[1])),
           "w_out1": _ca(A(ssm_w_out[0]))}
    ims = []
    for i in range(NC):
        g = i
        tok = slice(i * NT, (i + 1) * NT)
        cs = slice(i * 1536, (i + 1) * 1536)
        cols = np.concatenate([np.arange(g * 512, (g + 1) * 512), 4096 + np.arange(g * 512, (g + 1) * 512),
                               8192 + np.arange(g * 128, (g + 1) * 128), 9216 + np.arange(g * 128, (g + 1) * 128),
                               10240 + np.arange(g * 8, (g + 1) * 8), 10304 + np.arange(g * 8, (g + 1) * 8)])
        ch = np.concatenate([np.arange(g * 512, (g + 1) * 512), 4096 + np.arange(g * 128, (g + 1) * 128), 5120 + np.arange(g * 128, (g + 1) * 128)])
        ixv = np.zeros((128, 128), np.uint32)
        for tb in range(16):
            for gg in range(8):
                ixv[:, tb * 8 + gg] = gg * S_ + i * NT + tb * 128 + np.arange(128)
        ims.append(dict(com, xT=_ca(x[0, tok].T), adaw=_ca(ada_w[:, :, cs]), adab=_ca(np.stack([_lay(ada_b[l, cs]) for l in range(2)])),
                        pos=_ca(positions[:, tok]), wg=_ca(w_in1[:, cols]), cw=_ca(cw_[:, ch].reshape(5, 6, 128).transpose(2, 1, 0)),
                        cb=_ca(cb_[ch].reshape(6, 128).T), dtb=_ca(dtb_[:, g * 8:(g + 1) * 8].T),
                        alog_f=_ca(alog_[0, g * 8:(g + 1) * 8][:, None]), alog_b=_ca(alog_[1, g * 8:(g + 1) * 8][:, None]),
                        dbc=_ca(np.repeat(dsk[g * 8:(g + 1) * 8], 64)[None]), ngbc=_ca(ngm[g * 512:(g + 1) * 512][None]), ix=ixv))
    return ims


def kernel(**inputs):
    ims = make_inputs(**inputs)
    nc = build_program()
    res = run_bass_kernel_spmd(nc, ims, core_ids=list(range(NCORES)))
    out = np.concatenate([res.results[i]["outT"].T for i in range(NCORES)], axis=0)[None]
    return _ca(out.astype(np.float32))

import math
ROPE_C1 = 6.28125
ROPE_C2 = 2 * math.pi - 6.28125


def bc_rows(ap, n):
    return ap.broadcast(0, n) if hasattr(ap, "broadcast") else ap[0:1, :].to_broadcast([n, ap.shape[1]])


def rope_tables(cx, pos_ap, c0, n, invf, invfT, bufs):
    S = cx.S
    (pt, ptT), (ang, angT), (kf, kfT), (ki, kiT), (cs, csT), (sn, snT), (a2, a2T) = bufs
    S.dma("sp", [], [ptT], lambda e: e.dma_start(out=pt[:, 0:n], in_=bc_rows(pos_ap[0:1, c0:c0 + n], 64)))
    S.op("dve", [ptT], [angT], lambda e: e.tensor_copy(ang[:, 0:n], pt[:, 0:n]))
    S.op("dve", [angT, invfT], [angT], lambda e: e.tensor_scalar(ang[:, 0:n], ang[:, 0:n], invf[:, 0:1], None, op0=ALU.mult))
    for (dst, dstT, shift) in ((sn, snT, 0.0), (cs, csT, math.pi / 2)):
        S.op("dve", [angT], [kfT], lambda e, shift=shift: e.tensor_scalar(kf[:, 0:n], ang[:, 0:n], 1.0 / (2 * math.pi), 0.5 + shift / (2 * math.pi), op0=ALU.mult, op1=ALU.add))
        S.op("dve", [kfT], [kiT], lambda e: e.tensor_copy(ki[:, 0:n], kf[:, 0:n]))
        S.op("dve", [kiT], [kfT], lambda e: e.tensor_copy(kf[:, 0:n], ki[:, 0:n]))
        S.op("dve", [kfT, angT], [a2T], lambda e: e.scalar_tensor_tensor(a2[:, 0:n], kf[:, 0:n], -ROPE_C1, ang[:, 0:n], op0=ALU.mult, op1=ALU.add))
        S.op("dve", [kfT, a2T], [a2T], lambda e, shift=shift: e.scalar_tensor_tensor(a2[:, 0:n], kf[:, 0:n], -ROPE_C2, a2[:, 0:n], op0=ALU.mult, op1=ALU.add))
        if shift:
            S.op("dve", [a2T], [a2T], lambda e, shift=shift: e.tensor_scalar(a2[:, 0:n], a2[:, 0:n], shift, None, op0=ALU.add))
        S.op("dve", [a2T], [kfT], lambda e: e.tensor_scalar(kf[:, 0:n], a2[:, 0:n], -math.pi, 2 * math.pi, op0=ALU.is_lt, op1=ALU.mult))
        S.op("dve", [kfT, a2T], [a2T], lambda e: e.tensor_tensor(a2[:, 0:n], a2[:, 0:n], kf[:, 0:n], op=ALU.add))
        S.op("dve", [a2T], [kfT], lambda e: e.tensor_scalar(kf[:, 0:n], a2[:, 0:n], math.pi, -2 * math.pi, op0=ALU.is_gt, op1=ALU.mult))
        S.op("dve", [kfT, a2T], [a2T], lambda e: e.tensor_tensor(a2[:, 0:n], a2[:, 0:n], kf[:, 0:n], op=ALU.add))
        S.op("act", [a2T], [dstT], lambda e, dst=dst: e.activation(out=dst[:, 0:n], in_=a2[:, 0:n], func=AF.Sin))


def phase_A0(cx, NT, xT_ap, mod_ap, nmg_ap, w_in_ap, sgug_ap, wsT_ap, bs_ap, qg_ap, kvg_ap, w_uq_ap, w_ukv_ap,
             pos_ap, invf_ap, aT_o, qT_o, kT_o, kpeT_o, v_o):
    S = cx.S
    D = 2048
    TS = 512
    mod, modT = mod_cols(cx, mod_ap)
    nmg, nmgT = load_small(cx, nmg_ap, [128, 16], F32, "nmg")
    gsc, gscT = cx.sb("gsc", [128, 16], F32)
    S.op("dve", [modT, nmgT], [gscT], lambda e: e.scalar_tensor_tensor(gsc[:], mod[:, 16:32], 1.0, nmg[:], op0=ALU.add, op1=ALU.mult))
    qg, qgT = load_small(cx, qg_ap, [128, 4], F32, "qg")
    kvg, kvgT = load_small(cx, kvg_ap, [128, 4], F32, "kvg")
    invf, invfT = load_small(cx, invf_ap, [64, 1], F32, "invf")
    gbc, gbcT = cx.sb("gbc", [128, 1024], F32)
    S.dma("sp", [], [gbcT], lambda e: e.dma_start(out=gbc[:], in_=bc_rows(sgug_ap, 128)))
    bsb, bsbT = cx.sb("bsb", [128, 1024], F32)
    S.dma("sp", [], [bsbT], lambda e: e.dma_start(out=bsb[:], in_=bc_rows(bs_ap, 128)))
    wsT, wsTT = cx.sb("wsT", [128, 8, 128], BF16)
    S.dma("pool", [], [wsTT], lambda e: e.dma_start(out=wsT[:], in_=wsT_ap))
    wuq, wuqT = cx.sb("wuq", [128, 4, 1536], BF16)
    S.dma("pool", [], [wuqT], lambda e: e.dma_start(out=wuq[:], in_=w_uq_ap.rearrange("(c p) m -> p c m", p=128)))
    wukv, wukvT = cx.sb("wukv", [128, 4, 2048], BF16)
    S.dma("pool", [], [wukvT], lambda e: e.dma_start(out=wukv[:], in_=w_ukv_ap.rearrange("(c p) m -> p c m", p=128)))
    wk, wkT = cx.sb("wk", [128, 16, 64], BF16)
    S.dma("pool", [], [wkT], lambda e: e.dma_start(out=wk[:], in_=w_in_ap[:, 3072:3136].rearrange("(c p) m -> p c m", p=128)))
    wkr, wkrT = cx.sb("wkr", [128, 16, 64], BF16)
    S.op("dve", [wkT], [wkrT], lambda e: e.tensor_scalar(wkr[:, :, 0:32], wk[:, :, 32:64], -1.0, None, op0=ALU.mult))
    S.op("dve", [wkT, wkrT], [wkrT], lambda e: e.tensor_copy(wkr[:, :, 32:64], wk[:, :, 0:32]))
    wuqr, wuqrT = cx.sb("wuqr", [128, 4, 8, 64], BF16)
    wuq4 = wuq[:].rearrange("p c (h m) -> p c h m", m=192)
    for kc in range(4):
        S.op("dve", [wuqT, wuqrT], [wuqrT], lambda e, kc=kc: e.tensor_scalar(wuqr[:, kc, :, 0:32], wuq4[:, kc, :, 160:192], -1.0, None, op0=ALU.mult))
        S.op("dve", [wuqT, wuqrT], [wuqrT], lambda e, kc=kc: e.tensor_copy(wuqr[:, kc, :, 32:64], wuq4[:, kc, :, 128:160]))
    wukv4 = wukv[:].rearrange("p c (h m) -> p c h m", m=256)

    xb, _ = cx.sb("xb", [128, 16, TS], F32)
    xbT = S.tiles_n(16, "xb")
    hT, _ = cx.sb("hT", [128, 16, TS], BF16)
    hTT = S.tiles_n(16, "hT")
    uT, _ = cx.sb("uT", [128, 8, TS], BF16)
    uTT = S.tiles_n(8, "uT")
    vtm, _ = cx.sb("vtm", [128, 4, 1024], BF16)
    vtmT = S.tiles_n(4, "vtm")
    lat, _ = cx.sb("lat", [128, 8, TS], BF16)
    latT = S.tiles_n(8, "lat")
    wbufs = [cx.sb("wb", [128, 4096], BF16) for _ in range(2)]
    sqbufs = [cx.sb("sq", [128, TS], BF16) for _ in range(2)]
    rstd, rstdT = cx.sb("rstd", [128, TS], F32)
    rq, rqT = cx.sb("rq", [128, TS], F32)
    rkv, rkvT = cx.sb("rkv", [128, TS], F32)
    tmps = [cx.sb("tmp", [128, 1024], F32) for _ in range(2)]
    vn, vnT = cx.sb("vn", [128, 1024], BF16)
    junk, junkT = cx.sb("junk", [128, 1024], BF16)
    st, stT = cx.sb("st", [128, 8], F32)
    ob, obT = cx.sb("ob", [128, 8, TS], BF16)
    qpeo, qpeoT = cx.sb("qpeo", [64, 8, TS], BF16)
    kpeo, kpeoT = cx.sb("kpeo", [64, TS], BF16)
    vo, _ = cx.sb("vo", [128, 4, 1024], BF16)
    voT = S.tiles_n(4, "vo")
    rb = [cx.sb(n_, [64, TS], I32 if n_ in ("pt", "ki") else F32) for n_ in ("pt", "ang", "kf", "ki", "cs", "sn", "a2")]
    cs, csT = rb[4]
    sn, snT = rb[5]
    r1, r1T = cx.sb("r1", [64, TS], F32)
    r2, r2T = cx.sb("r2", [64, TS], F32)
    trr = [0]

    def tmp():
        trr[0] += 1
        return tmps[trr[0] % 2]

    def rope_out(psA, psAT, psB, psBT, out_ap, outT):
        S.op("dve", [psAT, csT], [r1T], lambda e: e.tensor_tensor(r1[:], psA, cs[:], op=ALU.mult))
        S.op("dve", [psBT, snT], [r2T], lambda e: e.tensor_tensor(r2[:], psB, sn[:], op=ALU.mult))
        S.op("dve", [r1T, r2T], [outT], lambda e: e.tensor_tensor(out_ap, r1[:], r2[:], op=ALU.add))

    def do_tile(ti):
        c0 = ti * TS
        for j in range(4):
            S.dma("sp", [], xbT[4 * j:4 * j + 4], lambda e, j=j: e.dma_start(
                out=xb[:, 4 * j:4 * j + 4, :], in_=xT_ap[512 * j:512 * (j + 1), c0:c0 + TS].rearrange("(c p) n -> p c n", p=128)))
        rope_tables(cx, pos_ap, c0, TS, invf, invfT, rb)
        norm_T(cx, lambda c: (xb[:, c, :], xbT[c]), 16, TS, D, sqbufs, rstd, rstdT)
        for c in range(16):
            tp, tpT = tmp()
            S.op("dve", [xbT[c], rstdT, gscT], [tpT], lambda e, c=c, tp=tp: e.scalar_tensor_tensor(
                tp[:, 0:TS], xb[:, c, :], gsc[:, c:c + 1], rstd[:], op0=ALU.mult, op1=ALU.mult))
            S.op("act", [tpT, modT], [hTT[c]], lambda e, c=c, tp=tp: e.activation(
                out=hT[:, c, :], in_=tp[:, 0:TS], func=AF.Identity, bias=mod[:, c:c + 1]))
        rhs_h = lambda k, t: (hT[:, k, :], hTT[k])
        gemm_T(cx, w_in_ap, D, [(m * 128, 128) for m in range(8)], rhs_h, 1,
               lambda ci, t, ps, psT: S.op("act", [psT], [uTT[ci]], lambda e: e.activation(out=uT[:, ci, :], in_=ps, func=AF.Gelu_apprx_tanh)), wbufs)
        for cb in range(4):
            wb, wbT = wbufs[cx._wrr % len(wbufs)]
            cx._wrr += 1
            wv = wb[:, 0:16 * 256].rearrange("p (c m) -> p c m", m=256)
            S.dma("pool", [], [wbT], lambda e, wv=wv, cb=cb: e.dma_start(
                out=wv, in_=w_in_ap[:, 1024 + cb * 256:1024 + (cb + 1) * 256].rearrange("(c p) m -> p c m", p=128)))
            for tb in range(4):
                (b,) = cx.next_ps(1)
                ps = cx.ps[b]

                def mm(e, ps=ps, wv=wv, tb=tb):
                    ins = None
                    for k in range(16):
                        ins = e.matmul(ps[:, 0:256], hT[:, k, tb * 128:(tb + 1) * 128], wv[:, k, :], start=(k == 0), stop=(k == 15))
                    return ins
                S.op("pe", [wbT] + hTT, [cx.psT[b]], mm)
                S.op("act", [cx.psT[b]], [vtmT[tb]], lambda e, ps=ps, tb=tb, cb=cb: e.activation(
                    out=vtm[:, tb, cb * 256:(cb + 1) * 256], in_=ps[:, 0:256], func=AF.Gelu_apprx_tanh))
        for tb in range(4):
            S.op("dve", [], [stT], lambda e: e.memset(st[:], 0.0))
            S.op("act", [vtmT[tb], stT], [junkT, stT], lambda e, tb=tb: e.activation(out=junk[:], in_=vtm[:, tb, :], func=AF.Identity, accum_out=st[:, 0:1]))
            S.op("act", [vtmT[tb], stT], [junkT, stT], lambda e, tb=tb: e.activation(out=junk[:], in_=vtm[:, tb, :], func=AF.Square, accum_out=st[:, 1:2]))
            S.op("dve", [stT], [stT], lambda e: e.tensor_scalar(st[:, 2:4], st[:, 0:2], 1.0 / 1024, None, op0=ALU.mult))
            S.op("dve", [stT], [stT], lambda e: e.tensor_tensor(st[:, 4:5], st[:, 2:3], st[:, 2:3], op=ALU.mult))
            S.op("dve", [stT], [stT], lambda e: e.tensor_tensor(st[:, 5:6], st[:, 3:4], st[:, 4:5], op=ALU.subtract))
            S.op("act", [stT], [stT], lambda e: e.activation(out=st[:, 6:7], in_=st[:, 5:6], func=AF.Sqrt, bias=EPS))
            S.op("dve", [stT], [stT], lambda e: e.reciprocal(st[:, 7:8], st[:, 6:7]))
            tp, tpT = tmp()
            S.op("dve", [vtmT[tb], stT], [tpT], lambda e, tb=tb, tp=tp: e.tensor_scalar(
                tp[:], vtm[:, tb, :], st[:, 2:3], st[:, 7:8], op0=ALU.subtract, op1=ALU.mult))
            S.op("dve", [tpT, gbcT], [vnT], lambda e, tp=tp: e.tensor_tensor(vn[:], tp[:], gbc[:], op=ALU.mult))
            for half in range(2):
                (b,) = cx.next_ps(1)
                ps = cx.ps[b]

                def mm(e, ps=ps, half=half):
                    ins = None
                    for gg in range(4):
                        g = half * 4 + gg
                        ins = e.matmul(ps[:, gg * 128:(gg + 1) * 128], vn[:, g * 128:(g + 1) * 128], wsT[:, g, :], start=True, stop=True)
                    return ins
                S.op("pe", [vnT, wsTT], [cx.psT[b]], mm)
                tp, tpT = tmp()
                S.op("dve", [cx.psT[b], bsbT], [tpT], lambda e, ps=ps, tp=tp, half=half: e.tensor_tensor(
                    tp[:, 0:512], ps[:, :], bsb[:, half * 512:(half + 1) * 512], op=ALU.add))
                S.op("dve", [tpT] + uTT[half * 4:half * 4 + 4], uTT[half * 4:half * 4 + 4], lambda e, tp=tp, half=half, tb=tb: e.tensor_tensor(
                    uT[:, half * 4:half * 4 + 4, tb * 128:(tb + 1) * 128], tp[:, 0:512].rearrange("p (g i) -> p g i", i=128),
                    uT[:, half * 4:half * 4 + 4, tb * 128:(tb + 1) * 128], op=ALU.mult))
        S.dma("sp", uTT, [], lambda e: e.dma_start(out=aT_o[:, c0:c0 + TS].rearrange("(c p) n -> p c n", p=128), in_=uT[:]))

        def ev_lat(ci, t, ps, psT):
            S.op("act", [psT], [latT[ci]], lambda e: e.activation(out=lat[:, ci, :], in_=ps, func=AF.Copy))
            sq, sqT = sqbufs[ci % 2]
            S.op("act", [psT], [sqT], lambda e: e.activation(out=sq[:, 0:TS], in_=ps, func=AF.Square))
            bk = 6 if ci < 4 else 7
            S.op("pe", [cx.ones_bfT, sqT], [cx.psT[bk]], lambda e: e.matmul(
                cx.ps[bk][:, :], cx.ones_bf[:, :], sq[:, 0:TS], start=(ci % 4 == 0), stop=(ci % 4 == 3)))
        gemm_T(cx, w_in_ap, D, [(2048 + m * 128, 128) for m in range(8)], rhs_h, 1, ev_lat, wbufs)
        rstd_from_ps(cx, 6, rq[:], rqT, 512)
        rstd_from_ps(cx, 7, rkv[:], rkvT, 512)
        for m in range(8):
            g_, gT_, r_, rT_ = (qg, qgT, rq, rqT) if m < 4 else (kvg, kvgT, rkv, rkvT)
            S.op("dve", [latT[m], gT_, rT_], [latT[m]], lambda e, m=m, g_=g_, r_=r_: e.scalar_tensor_tensor(
                lat[:, m, :], lat[:, m, :], g_[:, m % 4:m % 4 + 1], r_[:], op0=ALU.mult, op1=ALU.mult))
        (bA,) = cx.next_ps(1)
        (bB,) = cx.next_ps(1)

        def mmk(e, w, b):
            ins = None
            for k in range(16):
                ins = e.matmul(cx.ps[b][0:64, :], w[:, k, :], hT[:, k, :], start=(k == 0), stop=(k == 15))
            return ins
        S.op("pe", [wkT] + hTT, [cx.psT[bA]], lambda e: mmk(e, wk, bA))
        S.op("pe", [wkrT] + hTT, [cx.psT[bB]], lambda e: mmk(e, wkr, bB))
        rope_out(cx.ps[bA][0:64, :], cx.psT[bA], cx.ps[bB][0:64, :], cx.psT[bB], kpeo[:], kpeoT)
        S.dma("sp", [kpeoT], [], lambda e: e.dma_start(out=kpeT_o[:, c0:c0 + TS], in_=kpeo[:]))
        for h in range(8):
            (b,) = cx.next_ps(1)

            def mmq(e, b=b, h=h):
                ins = None
                for kc in range(4):
                    ins = e.matmul(cx.ps[b][:, :], wuq[:, kc, h * 192:h * 192 + 128], lat[:, kc, :], start=(kc == 0), stop=(kc == 3))
                return ins
            S.op("pe", [wuqT] + latT[0:4], [cx.psT[b]], mmq)
            S.op("act", [cx.psT[b]], [obT], lambda e, b=b, h=h: e.activation(out=ob[:, h, :], in_=cx.ps[b][:, :], func=AF.Copy))
            (bA,) = cx.next_ps(1)
            (bB,) = cx.next_ps(1)

            def mmp(e, b, w):
                ins = None
                for kc in range(4):
                    ins = e.matmul(cx.ps[b][0:64, :], w(kc), lat[:, kc, :], start=(kc == 0), stop=(kc == 3))
                return ins
            S.op("pe", [wuqT] + latT[0:4], [cx.psT[bA]], lambda e, bA=bA, h=h: mmp(e, bA, lambda kc: wuq[:, kc, h * 192 + 128:h * 192 + 192]))
            S.op("pe", [wuqrT] + latT[0:4], [cx.psT[bB]], lambda e, bB=bB, h=h: mmp(e, bB, lambda kc: wuqr[:, kc, h, :]))
            rope_out(cx.ps[bA][0:64, :], cx.psT[bA], cx.ps[bB][0:64, :], cx.psT[bB], qpeo[:, h, :], qpeoT)
        S.dma("sp", [obT], [], lambda e: e.dma_start(out=qT_o[:, 0:128, c0:c0 + TS].rearrange("h p n -> p h n"), in_=ob[:]))
        S.dma("sp", [qpeoT], [], lambda e: e.dma_start(out=qT_o[:, 128:192, c0:c0 + TS].rearrange("h p n -> p h n"), in_=qpeo[:]))
        for h in range(8):
            (b,) = cx.next_ps(1)

            def mmkn(e, b=b, h=h):
                ins = None
                for kc in range(4):
                    ins = e.matmul(cx.ps[b][:, :], wukv[:, kc, h * 256:h * 256 + 128], lat[:, 4 + kc, :], start=(kc == 0), stop=(kc == 3))
                return ins
            S.op("pe", [wukvT] + latT[4:8], [cx.psT[b]], mmkn)
            S.op("act", [cx.psT[b], obT], [obT], lambda e, b=b, h=h: e.activation(out=ob[:, h, :], in_=cx.ps[b][:, :], func=AF.Copy))
        S.dma("sp", [obT], [], lambda e: e.dma_start(out=kT_o[:, :, c0:c0 + TS].rearrange("h p n -> p h n"), in_=ob[:]))
        for tb in range(4):
            for half in range(2):
                (b,) = cx.next_ps(1)

                def mmv(e, b=b, tb=tb, half=half):
                    ins = None
                    for kc in range(4):
                        ins = e.matmul(cx.ps[b][:, :], lat[:, 4 + kc, tb * 128:(tb + 1) * 128], wukv4[:, kc, half * 4:half * 4 + 4, 128:256], start=(kc == 0), stop=(kc == 3))
                    return ins
                S.op("pe", [wukvT] + latT[4:8], [cx.psT[b]], mmv)
                S.op("dve", [cx.psT[b]], [voT[tb]], lambda e, b=b, tb=tb, half=half: e.tensor_copy(vo[:, tb, half * 512:(half + 1) * 512], cx.ps[b][:, :]))
        S.dma("sp", voT, [], lambda e: e.dma_start(out=v_o[c0:c0 + TS, :].rearrange("(t p) f -> p t f", p=128), in_=vo[:]))

    for ti_ in range(NT // TS):
        do_tile(ti_)

def phase_A1a(cx, SQ, hT_ap, wg_ap, cw_ap, cb_ap, dtb_ap, zs_o, xraw_o, xbc_o, dt_o):
    S = cx.S
    TS = 512
    wg, _ = cx.sb("wg", [128, 16, 1296], BF16)
    wgT = S.tiles_n(4, "wg")
    for j in range(4):
        S.dma("pool", [], [wgT[j]], lambda e, j=j: e.dma_start(out=wg[:, 4 * j:4 * j + 4, :], in_=wg_ap[512 * j:512 * (j + 1), :].rearrange("(c p) m -> p c m", p=128)))
    cw, cwT = load_small(cx, cw_ap, [128, 6, 5], F32, "cw")
    cb, cbT = load_small(cx, cb_ap, [128, 6], F32, "cb")
    dtb, dtbT = load_small(cx, dtb_ap, [8, 2], F32, "dtb")
    hb = [cx.sb("hb", [128, 16, TS], BF16) for _ in range(2)]
    zo = [cx.sb("zo", [128, 4, TS], F32) for _ in range(2)]
    xr = [cx.sb("xr", [128, 6, TS], F32) for _ in range(2)]
    dto = [cx.sb("dto", [8, 2, TS], F32) for _ in range(2)]
    e1, e1T = cx.sb("e1", [8, TS], F32)
    xrawT = S.tiles_n(SQ // TS, "xraw_dram")

    def loadH(ti):
        c0 = ti * TS
        H, HT = hb[ti % 2]
        hsrc = hT_ap(ti) if callable(hT_ap) else hT_ap[:, c0:c0 + TS]
        S.dma("sp", [], [HT], lambda e: e.dma_start(out=H[:], in_=hsrc.rearrange("(c p) n -> p c n", p=128)))

    def tile1(ti):
        c0 = ti * TS
        H, HT = hb[ti % 2]
        Z, ZT = zo[ti % 2]
        X, XT = xr[ti % 2]
        DT, DTT = dto[ti % 2]
        for tb in range(4):
            (b,) = cx.next_ps(1)

            def mmz(e, b=b, tb=tb):
                ins = None
                for k in range(16):
                    ins = e.matmul(cx.ps[b][:, :], H[:, k, tb * 128:(tb + 1) * 128], wg[:, k, 0:512], start=(k == 0), stop=(k == 15))
                return ins
            S.op("pe", wgT + [HT], [cx.psT[b]], mmz)
            S.op("act", [cx.psT[b]], [ZT], lambda e, b=b, tb=tb: e.activation(out=Z[:, tb, :], in_=cx.ps[b][:, :], func=AF.Silu))
        for m in range(4, 10):
            (b,) = cx.next_ps(1)

            def mm(e, b=b, m=m):
                ins = None
                for k in range(16):
                    ins = e.matmul(cx.ps[b][:, :], wg[:, k, m * 128:(m + 1) * 128], H[:, k, :], start=(k == 0), stop=(k == 15))
                return ins
            S.op("pe", wgT + [HT], [cx.psT[b]], mm)
            S.op("dve", [cx.psT[b]], [XT], lambda e, b=b, m=m: e.tensor_copy(X[:, m - 4, :], cx.ps[b][:, :]))
        for d in range(2):
            (b,) = cx.next_ps(1)

            def mmd(e, b=b, d=d):
                ins = None
                for k in range(16):
                    ins = e.matmul(cx.ps[b][0:8, :], wg[:, k, 1280 + 8 * d:1288 + 8 * d], H[:, k, :], start=(k == 0), stop=(k == 15))
                return ins
            S.op("pe", wgT + [HT], [cx.psT[b]], mmd)
            S.op("act", [cx.psT[b], dtbT], [e1T], lambda e, b=b, d=d: e.activation(out=e1[:], in_=cx.ps[b][0:8, :], func=AF.Exp, bias=dtb[:, d:d + 1]))
            S.op("act", [e1T], [DTT], lambda e, d=d: e.activation(out=DT[:, d, :], in_=e1[:], func=AF.Ln, bias=1.0))
        S.dma("sp", [ZT], [], lambda e: e.dma_start(out=zs_o[c0:c0 + TS, :].rearrange("(t p) f -> p t f", p=128), in_=Z[:]))
        S.dma("sp", [XT], [xrawT[ti]], lambda e: e.dma_start(out=xraw_o[:, c0:c0 + TS].rearrange("(c p) n -> p c n", p=128), in_=X[:]))
        S.dma("sp", [DTT], [], lambda e: e.dma_start(out=dt_o[:, :, c0:c0 + TS].rearrange("d h n -> h d n"), in_=DT[:]))
    for ti in range(SQ // TS):
        loadH(ti)
        tile1(ti)
    S.flush()
    xw = [cx.sb("xw", [128, 6, TS + 4], F32) for _ in range(2)]
    acc = [cx.sb("acc", [128, TS], F32) for _ in range(2)]
    xo = [cx.sb("xo", [128, 6, TS], F32) for _ in range(2)]

    def tile2(ti):
        c0 = ti * TS
        W, WT = xw[ti % 2]
        O, OT = xo[ti % 2]
        lo = max(c0 - 2, 0)
        hi = min(c0 + TS + 2, SQ)
        if lo != c0 - 2 or hi != c0 + TS + 2:
            S.op("dve", [], [WT], lambda e: e.memset(W[:], 0.0))
        d0 = lo - (c0 - 2)
        S.dma("sp", [], [WT], lambda e: e.dma_start(out=W[:, :, d0:d0 + hi - lo], in_=xraw_o[:, lo:hi].rearrange("(c p) n -> p c n", p=128)))
        for c in range(6):
            A, AT = acc[c % 2]
            S.op("dve", [WT, cwT], [AT], lambda e, c=c, A=A: e.tensor_scalar(A[:], W[:, c, 0:TS], cw[:, c, 0:1], None, op0=ALU.mult))
            for k in range(1, 5):
                S.op("dve", [WT, cwT, AT], [AT], lambda e, c=c, k=k, A=A: e.scalar_tensor_tensor(A[:], W[:, c, k:k + TS], cw[:, c, k:k + 1], A[:], op0=ALU.mult, op1=ALU.add))
            S.op("act", [AT, cbT], [OT], lambda e, c=c, A=A: e.activation(out=O[:, c, :], in_=A[:], func=AF.Silu, bias=cb[:, c:c + 1]))
        S.dma("sp", [OT], [], lambda e: e.dma_start(out=xbc_o[:, c0:c0 + TS].rearrange("(c p) n -> p c n", p=128), in_=O[:]))
    for ti in range(SQ // TS):
        tile2(ti)


def make_scan(cx, SQ, xbcT_ap, xtm_ap, btm_ap, dtT_ap, alog_ap, y_o, fwd, write_x, pb):
    S = cx.S
    SC = 512
    first = True
    bM, bA, bY, bG = pb, pb + 1, pb + 2, pb + 3
    mxT = mdT = mcT = mbT = cx.psT[bM]
    al, alT = load_small(cx, alog_ap, [8, 1], F32, "al")
    a, aT = cx.sb("a", [8, 1], F32)
    S.op("act", [alT], [aT], lambda e: e.activation(out=a[:], in_=al[:], func=AF.Exp))
    S.op("dve", [aT], [aT], lambda e: e.tensor_scalar(a[:], a[:], -1.0, None, op0=ALU.mult))
    ident, identT = cx.sb("ident", [128, 128], F32)
    S.op("dve", [], [identT], lambda e: e.memset(ident[:], 0.0))
    S.op("pool", [identT], [identT], lambda e: e.affine_select(out=ident[:], in_=ident[:], pattern=[[-1, 128]], compare_op=ALU.not_equal, fill=1.0, base=0, channel_multiplier=1))
    maskL, maskLT = cx.sb("maskL", [128, 128], F32)
    S.op("dve", [], [maskLT], lambda e: e.memset(maskL[:], 1.0))
    if fwd:
        S.op("pool", [maskLT], [maskLT], lambda e: e.affine_select(out=maskL[:], in_=maskL[:], pattern=[[1, 128]], compare_op=ALU.is_ge, fill=0.0, base=0, channel_multiplier=-1))
    else:
        S.op("pool", [maskLT], [maskLT], lambda e: e.affine_select(out=maskL[:], in_=maskL[:], pattern=[[-1, 128]], compare_op=ALU.is_ge, fill=0.0, base=0, channel_multiplier=1))
    colL = 127 if fwd else 0
    XTs = [cx.sb("XTs", [128, 4, SC], F32) for _ in range(2)] if first else None
    BTf = [cx.sb("BTf", [128, SC], F32) for _ in range(2)] if first else None
    tmp8, tmp8T = cx.sb("tmp8", [8, 128], F32)
    sel, selT = cx.sb("sel", [8, 8, 128], F32)
    S.op("dve", [], [selT], lambda e: e.memset(sel[:], 0.0))
    S.op("pool", [selT], [selT], lambda e: e.affine_select(out=sel[:], in_=sel[:], pattern=[[-1, 8], [0, 128]], compare_op=ALU.not_equal, fill=1.0, base=0, channel_multiplier=1))
    zc, zcT = cx.sb("zc", [128, 1], F32)
    S.op("dve", [], [zcT], lambda e: e.memset(zc[:], 0.0))
    ones8, ones8T = cx.sb("ones8", [8, 128], F32)
    S.op("dve", [], [ones8T], lambda e: e.memset(ones8[:], 1.0))
    st32, st32T = cx.sb("st32", [128, 512], F32)
    S.op("dve", [], [st32T], lambda e: e.memset(st32[:], 0.0))
    stbf, stbfT = cx.sb("stbf", [128, 512], BF16)
    S.op("dve", [], [stbfT], lambda e: e.memset(stbf[:], 0.0))
    BTs = [cx.sb("BTs", [128, SC], BF16) for _ in range(2)]
    CTs = [cx.sb("CTs", [128, SC], BF16) for _ in range(2)]
    xs = [cx.sb("xs", [128, 4, 512], F32) for _ in range(2)]
    bs = [cx.sb("bs", [128, 4, 128], BF16) for _ in range(2)]
    dts = [cx.sb("dts", [128, 4, 8], F32) for _ in range(2)]
    dtTs = [cx.sb("dtTs", [8, SC], F32) for _ in range(2)]
    yo = [cx.sb("yo", [128, 4, 512], F32) for _ in range(2)]
    dta, dtaT = cx.sb("dta", [8, 128], F32)
    cumT, cumTT = cx.sb("cumT", [8, 128], F32)
    cum, cumtT = cx.sb("cum", [128, 8], F32)
    cbm, cbmT = cx.sb("cbm", [128, 128], F32)
    seg, segT = cx.sb("seg", [128, 8, 128], F32)
    wT, wTT = cx.sb("wT", [128, 8, 128], BF16)
    sm, smT = cx.sb("sm", [128, 5, 8], F32)
    xdt, xdtT = cx.sb("xdt", [128, 512], BF16)
    xe, xeT = cx.sb("xe", [128, 512], BF16)
    gy, gyT = cx.sb("gy", [128, 512], F32)

    def chunk(sc, k, bufs):
        (B_, BT_), (C_, CT_), (X_, XT_), (Bm, BmT), (D_, DT_), (DTr, DTrT), (Y_, YT_) = bufs
        cs = slice(k * 128, (k + 1) * 128)
        S.op("dve", [DTrT, aT], [dtaT], lambda e: e.tensor_scalar(dta[:], DTr[:, cs], a[:, 0:1], None, op0=ALU.mult))
        S.op("dve", [dtaT, ones8T], [cumTT], lambda e: e.tensor_tensor_scan(cumT[:], ones8[:], dta[:], 0.0, op0=ALU.mult, op1=ALU.add))
        if not fwd:
            S.op("dve", [dtaT, cumTT], [tmp8T], lambda e: e.tensor_tensor(tmp8[:], dta[:], cumT[:], op=ALU.subtract))
            S.op("dve", [tmp8T, cumTT], [tmp8T], lambda e: e.tensor_scalar(tmp8[:], tmp8[:], cumT[:, 127:128], None, op0=ALU.add))
            S.op("dve", [tmp8T], [cumTT], lambda e: e.tensor_copy(cumT[:], tmp8[:]))
        pA0, pA1, pY, pG, pS = bA, bG, bY, bG, bA
        S.op("pe", [DTrT, identT], [mdT], lambda e: e.transpose(cx.ps[bM][:, 8:16], DTr[:, cs], ident[0:8, 0:8]))
        S.op("dve", [mdT], [DT_], lambda e: e.tensor_copy(D_[:, k, :], cx.ps[bM][:, 8:16]))
        if first:
            XT4, XT4T = XTs[sc % 2]
            BF_, BFT_ = BTf[sc % 2]

            def trx(e):
                ins = None
                for c in range(4):
                    ins = e.transpose(cx.ps[pY][:, c * 128:(c + 1) * 128], XT4[:, c, cs], ident[:])
                return ins
            S.op("pe", [XT4T, identT], [cx.psT[pY]], trx)
            S.op("act", [cx.psT[pY]], [XT_], lambda e: e.activation(out=X_[:, k, :], in_=cx.ps[pY][:, :], func=AF.Copy))
            S.op("pe", [BFT_, identT], [mbT], lambda e: e.transpose(cx.ps[bM][:, 256:384], BF_[:, cs], ident[:]))
            S.op("act", [mbT], [BmT], lambda e: e.activation(out=Bm[:, k, :], in_=cx.ps[bM][:, 256:384], func=AF.Copy))
        S.op("pe", [cumTT, identT], [mxT], lambda e: e.transpose(cx.ps[bM][:, 0:8], cumT[:, :], ident[0:8, 0:8]))
        S.op("dve", [mxT], [cumtT], lambda e: e.tensor_copy(cum[:], cx.ps[bM][:, 0:8]))

        def mmA(e, half):
            ins = None
            for hh in range(4):
                ins = e.matmul(cx.ps[(pA0, pA1)[half]][:, hh * 128:(hh + 1) * 128], sel[:, half * 4 + hh, :], cumT[:, :], start=True, stop=True)
            return ins
        S.op("pe", [selT, cumTT], [cx.psT[pA0]], lambda e: mmA(e, 0))
        S.op("pe", [selT, cumTT], [cx.psT[pA1]], lambda e: mmA(e, 1))
        pAs = (pA0, pA1)
        S.op("pe", [BT_, CT_], [mcT], lambda e: e.matmul(cx.ps[bM][:, 128:256], B_[:, cs], C_[:, cs], start=True, stop=True))
        S.op("dve", [mcT, maskLT], [cbmT], lambda e: e.tensor_tensor(cbm[:], cx.ps[bM][:, 128:256], maskL[:], op=ALU.mult))
        for half in range(2):
            S.op("dve", [cx.psT[pAs[half]], cumtT, segT], [segT], lambda e, half=half: e.tensor_tensor(
                seg[:, half * 4:half * 4 + 4, :], cx.ps[pAs[half]][:, :].rearrange("p (h i) -> p h i", i=128),
                cum[:, half * 4:half * 4 + 4].unsqueeze(2).to_broadcast([128, 4, 128]), op=ALU.subtract))
        S.op("dve", [segT], [segT], lambda e: e.tensor_scalar(seg[:], seg[:], 0.0, None, op0=ALU.min))
        for half in range(2):
            S.op("dve", [cx.psT[pAs[half]]], [smT], lambda e, half=half: e.tensor_copy(
                sm[:, 0, half * 4:half * 4 + 4], cx.ps[pAs[half]][:, :].rearrange("p (h i) -> p h i", i=128)[:, :, colL]))
        S.op("act", [segT], [segT], lambda e: e.activation(out=seg[:], in_=seg[:], func=AF.Exp))
        S.op("dve", [segT, cbmT], [wTT], lambda e: e.tensor_tensor(wT[:], seg[:], cbm[:].unsqueeze(1).to_broadcast([128, 8, 128]), op=ALU.mult))
        S.op("dve", [smT, cumtT], [smT], lambda e: e.tensor_tensor(sm[:, 1, :], sm[:, 0, :], cum[:], op=ALU.subtract))
        S.op("act", [smT], [smT], lambda e: e.activation(out=sm[:, 1, :], in_=sm[:, 1, :], func=AF.Exp))
        S.op("act", [smT], [smT], lambda e: e.activation(out=sm[:, 2, :], in_=sm[:, 0, :], func=AF.Exp))
        S.op("act", [smT, cumtT], [smT], lambda e: e.activation(out=sm[:, 3, :], in_=cum[:], func=AF.Exp))
        S.op("dve", [smT, DT_], [smT], lambda e: e.tensor_tensor(sm[:, 4, :], sm[:, 1, :], D_[:, k, :], op=ALU.mult))
        hv = lambda ap: ap.rearrange("p (h q) -> p h q", q=64)
        S.op("dve", [XT_, DT_], [xdtT], lambda e: e.tensor_tensor(hv(xdt[:]), hv(X_[:, k, :]), D_[:, k, :].unsqueeze(2).to_broadcast([128, 8, 64]), op=ALU.mult))
        S.op("dve", [XT_, smT], [xeT], lambda e: e.tensor_tensor(hv(xe[:]), hv(X_[:, k, :]), sm[:, 4, :].unsqueeze(2).to_broadcast([128, 8, 64]), op=ALU.mult))

        def mmY(e):
            ins = None
            for h in range(8):
                ins = e.matmul(cx.ps[pY][:, h * 64:(h + 1) * 64], wT[:, h, :], xdt[:, h * 64:(h + 1) * 64], start=True, stop=True)
            return ins
        S.op("pe", [wTT, xdtT], [cx.psT[pY]], mmY)
        S.op("pe", [CT_, stbfT], [cx.psT[pG]], lambda e: e.matmul(cx.ps[pG][:, :], C_[:, cs], stbf[:], start=True, stop=True))
        for h in range(8):
            hs = slice(h * 64, (h + 1) * 64)
            S.op("act", [cx.psT[pG], smT], [gyT], lambda e, h=h, hs=hs: e.activation(out=gy[:, hs], in_=cx.ps[pG][:, hs], func=AF.Copy, scale=sm[:, 3, h:h + 1]))
        S.op("dve", [gyT, cx.psT[pY]], [YT_], lambda e: e.tensor_tensor(Y_[:, k, :], gy[:], cx.ps[pY][:, :], op=ALU.add))
        S.op("pe", [BmT, xeT], [cx.psT[pS]], lambda e: e.matmul(cx.ps[pS][:, :], Bm[:, k, :], xe[:], start=True, stop=True))
        S.op("dve", [st32T, smT], [st32T], lambda e: e.tensor_tensor(hv(st32[:]), hv(st32[:]), sm[:, 2, :].unsqueeze(2).to_broadcast([128, 8, 64]), op=ALU.mult))
        S.op("dve", [st32T, cx.psT[pS]], [st32T], lambda e: e.tensor_tensor(st32[:], st32[:], cx.ps[pS][:, :], op=ALU.add))
        S.op("act", [st32T], [stbfT], lambda e: e.activation(out=stbf[:], in_=st32[:], func=AF.Copy))

    def bufs_of(sc):
        return [BTs[sc % 2], CTs[sc % 2], xs[sc % 2], bs[sc % 2], dts[sc % 2], dtTs[sc % 2], yo[sc % 2]]

    def load(sc):
        c0 = sc * SC
        (B_, BT_), (C_, CT_), (X_, XT_), (Bm, BmT), (D_, DT_), (DTr, DTrT), (Y_, YT_) = bufs_of(sc)
        S.dma("pool", [], [BT_], lambda e: e.dma_start(out=B_[:], in_=xbcT_ap[512:640, c0:c0 + SC]))
        S.dma("pool", [], [CT_], lambda e: e.dma_start(out=C_[:], in_=xbcT_ap[640:768, c0:c0 + SC]))
        XT4, XT4T = XTs[sc % 2]
        BF_, BFT_ = BTf[sc % 2]
        S.dma("sp", [], [XT4T], lambda e: e.dma_start(out=XT4[:], in_=xbcT_ap[0:512, c0:c0 + SC].rearrange("(c p) n -> p c n", p=128)))
        S.dma("sp", [], [BFT_], lambda e: e.dma_start(out=BF_[:], in_=xbcT_ap[512:640, c0:c0 + SC]))
        S.dma("sp", [], [DTrT], lambda e: e.dma_start(out=DTr[:], in_=dtT_ap[:, c0:c0 + SC]))

    def do_chunk(sc, k):
        chunk(sc, k, bufs_of(sc))

    def store(sc):
        c0 = sc * SC
        (B_, BT_), (C_, CT_), (X_, XT_), (Bm, BmT), (D_, DT_), (DTr, DTrT), (Y_, YT_) = bufs_of(sc)
        S.dma("sp", [YT_], [], lambda e: e.dma_start(out=y_o[c0:c0 + SC, :].rearrange("(t p) f -> p t f", p=128), in_=Y_[:]))
        if write_x:
            S.dma("sp", [XT_], [], lambda e: e.dma_start(out=xtm_ap[c0:c0 + SC, :].rearrange("(t p) f -> p t f", p=128), in_=X_[:]))
    return load, do_chunk, store


def phase_A1b2(cx, SQ, xbcT_ap, xtm_ap, btm_ap, dt_ap, alog_f, alog_b, yf_o, yb_o):
    F = make_scan(cx, SQ, xbcT_ap, xtm_ap, btm_ap, dt_ap[0], alog_f, yf_o, True, True, 0)
    B = make_scan(cx, SQ, xbcT_ap, xtm_ap, btm_ap, dt_ap[1], alog_b, yb_o, False, False, 4)
    nsc = SQ // 512
    for s_ in range(nsc):
        sf, sb_ = s_, nsc - 1 - s_
        F[0](sf)
        B[0](sb_)
        for k in range(4):
            F[1](sf, k)
            B[1](sb_, 3 - k)
        F[2](sf)
        B[2](sb_)


def phase_A1c(cx, SQ, yf_ap, yb_ap, xtm_ap, ztm_ap, dbc_ap, ngbc_ap, yg_o):
    S = cx.S
    dbc, dbcT = cx.sb("dbc", [128, 512], F32)
    S.dma("sp", [], [dbcT], lambda e: e.dma_start(out=dbc[:], in_=bc_rows(dbc_ap, 128)))
    ngb, ngbT = cx.sb("ngb", [128, 512], F32)
    S.dma("sp", [], [ngbT], lambda e: e.dma_start(out=ngb[:], in_=bc_rows(ngbc_ap, 128)))
    bufs = [[cx.sb(n_, [128, 4, 512], F32) for n_ in ("yf", "yb", "x", "z")] for _ in range(2)]
    outs = [cx.sb("o", [128, 4, 512], BF16) for _ in range(2)]
    junk, junkT = cx.sb("junk", [128, 512], F32)
    st, stT = cx.sb("st", [128, 4], F32)

    def blk(ti):
        c0 = ti * 512
        (YF, YFT), (YB, YBT), (X, XT), (Z, ZT) = bufs[ti % 2]
        O, OT = outs[ti % 2]
        for (t_, tT_, src) in ((YF, YFT, yf_ap), (YB, YBT, yb_ap), (X, XT, xtm_ap), (Z, ZT, ztm_ap)):
            S.dma("sp", [], [tT_], lambda e, t_=t_, src=src: e.dma_start(out=t_[:], in_=src[c0:c0 + 512, :].rearrange("(t p) f -> p t f", p=128)))
        for k in range(4):
            S.op("dve", [XT, dbcT], [XT], lambda e, k=k: e.tensor_tensor(X[:, k, :], X[:, k, :], dbc[:], op=ALU.mult))
            S.op("dve", [XT, YFT], [XT], lambda e, k=k: e.tensor_tensor(X[:, k, :], X[:, k, :], YF[:, k, :], op=ALU.add))
            S.op("dve", [XT, YBT], [XT], lambda e, k=k: e.tensor_tensor(X[:, k, :], X[:, k, :], YB[:, k, :], op=ALU.add))
            S.op("dve", [XT, ZT], [XT], lambda e, k=k: e.tensor_tensor(X[:, k, :], X[:, k, :], Z[:, k, :], op=ALU.mult))
            S.op("dve", [], [stT], lambda e: e.memset(st[:], 0.0))
            S.op("act", [XT, stT], [junkT, stT], lambda e, k=k: e.activation(out=junk[:], in_=X[:, k, :], func=AF.Square, accum_out=st[:, 0:1]))
            S.op("act", [stT], [stT], lambda e: e.activation(out=st[:, 1:2], in_=st[:, 0:1], func=AF.Sqrt, scale=1.0 / 512, bias=EPS))
            S.op("dve", [stT], [stT], lambda e: e.reciprocal(st[:, 2:3], st[:, 1:2]))
            S.op("dve", [XT, stT, ngbT], [OT], lambda e, k=k: e.scalar_tensor_tensor(O[:, k, :], X[:, k, :], st[:, 2:3], ngb[:], op0=ALU.mult, op1=ALU.mult))
        S.dma("sp", [OT], [], lambda e: e.dma_start(out=yg_o[c0:c0 + 512, :].rearrange("(t p) f -> p t f", p=128), in_=O[:]))
    for ti in range(SQ // 512):
        blk(ti)

def phase_B0(cx, NQ, SK, qT_ap, kT_ap, kpeT_ap, v_ap, bT_o):
    S = cx.S
    NKT = SK // 128
    NQT = NQ // 512
    scale = 192 ** -0.5
    kpe, kpeT = cx.sb("kpe", [64, SK], BF16)
    gathered = len(kT_ap.shape) == 2
    if gathered:
        S.dma("sp", [], [kpeT], lambda e: e.dma_start(out=kpe[:].rearrange("p (r n) -> p r n", r=8), in_=kpeT_ap.rearrange("(r p) n -> p r n", r=8)))
    else:
        S.dma("sp", [], [kpeT], lambda e: e.dma_start(out=kpe[:], in_=kpeT_ap))
    kb = [cx.sb("kb", [128, SK], BF16) for _ in range(2)]
    vb = [cx.sb("vb", [128, NKT, 128], BF16) for _ in range(2)]
    qn = [cx.sb("qn", [128, NQ], BF16) for _ in range(2)]
    qp = [cx.sb("qp", [64, NQ], BF16) for _ in range(2)]
    pt = [cx.sb("pt", [128, 512], BF16) for _ in range(4)]
    rl, rlT = cx.sb("rl", [128, 512], F32)
    accs = [cx.sb("acc", [128, 512], F32) for _ in range(2)]
    ones32, ones32T = cx.sb("ones32", [128, 128], F32)
    S.op("dve", [], [ones32T], lambda e: e.memset(ones32[:], 1.0))
    ob = [cx.sb("ob", [128, 512], BF16) for _ in range(2)]
    def do_hq(h, qt, it, K_, KT_, V_, VT_, Qn, QnT, Qp, QpT):
        qs = slice(qt * 512, (qt + 1) * 512)
        bO, bL = (4, 5) if it % 2 == 0 else (6, 7)
        def sc(kt):
            b = kt % 4
            def mm(e, b=b, kt=kt):
                e.matmul(cx.ps[b][:, :], K_[:, kt * 128:(kt + 1) * 128], Qn[:, qs], start=True, stop=False)
                return e.matmul(cx.ps[b][:, :], kpe[:, kt * 128:(kt + 1) * 128], Qp[:, qs], start=False, stop=True)
            S.op("pe", [KT_, kpeT, QnT, QpT], [cx.psT[b]], mm)
            P_, PT_ = pt[b]
            S.op("act", [cx.psT[b]], [PT_], lambda e, b=b, P_=P_: e.activation(out=P_[:], in_=cx.ps[b][:, :], func=AF.Exp, scale=scale))

        AC, ACT_ = accs[it % 2]

        def pv(kt):
            P_, PT_ = pt[kt % 4]
            S.op("pe", [VT_, PT_], [cx.psT[bO]], lambda e, kt=kt, P_=P_: e.matmul(cx.ps[bO][:, :], V_[:, kt, :], P_[:], start=(kt == 0), stop=(kt == NKT - 1)))
            if kt == 0:
                S.op("dve", [PT_], [ACT_], lambda e, P_=P_: e.tensor_copy(AC[:], P_[:]))
            else:
                S.op("dve", [PT_, ACT_], [ACT_], lambda e, P_=P_: e.tensor_tensor(AC[:], AC[:], P_[:], op=ALU.add))
        sc(0)
        sc(1)
        for kt in range(NKT):
            if kt + 2 < NKT:
                sc(kt + 2)
            pv(kt)
        O_, OT_ = ob[it % 2]
        S.op("pe", [ACT_, ones32T], [cx.psT[bL]], lambda e: e.matmul(cx.ps[bL][:, :], ones32[:, :], AC[:], start=True, stop=True))
        S.op("dve", [cx.psT[bL]], [rlT], lambda e, bL=bL: e.reciprocal(rl[:], cx.ps[bL][:, :]))
        S.op("dve", [cx.psT[bO], rlT], [OT_], lambda e, bO=bO, O_=O_: e.tensor_tensor(O_[:], cx.ps[bO][:, :], rl[:], op=ALU.mult))
        S.dma("sp", [OT_], [], lambda e, O_=O_, h=h, qs=qs: e.dma_start(out=bT_o[h * 128:(h + 1) * 128, qs], in_=O_[:]))

    it = 0
    for h in range(8):
        K_, KT_ = kb[h % 2]
        V_, VT_ = vb[h % 2]
        Qn, QnT = qn[h % 2]
        Qp, QpT = qp[h % 2]
        if gathered:
            S.dma("sp", [], [KT_], lambda e, K_=K_, h=h: e.dma_start(out=K_[:].rearrange("p (r n) -> p r n", r=8), in_=kT_ap.rearrange("(r h p) n -> h p r n", r=8, h=8)[h]))
        else:
            S.dma("sp", [], [KT_], lambda e, K_=K_, h=h: e.dma_start(out=K_[:], in_=kT_ap[h]))
        S.dma("sp", [], [VT_], lambda e, V_=V_, h=h: e.dma_start(out=V_[:], in_=v_ap[:, h * 128:(h + 1) * 128].rearrange("(t p) d -> p t d", p=128)))
        S.dma("sp", [], [QnT], lambda e, Qn=Qn, h=h: e.dma_start(out=Qn[:], in_=qT_ap[h, 0:128, :]))
        S.dma("sp", [], [QpT], lambda e, Qp=Qp, h=h: e.dma_start(out=Qp[:], in_=qT_ap[h, 128:192, :]))
        for qt in range(NQT):
            do_hq(h, qt, it, K_, KT_, V_, VT_, Qn, QnT, Qp, QpT)
            it += 1


def phase_MOD(cx, cT_ap, adaw_ap, adab_ap, mod_o):
    S = cx.S
    c, cT = load_small(cx, cT_ap, [128, 16], F32, "c")
    cond, condT = cx.sb("cond", [128, 16], F32)
    S.op("act", [cT], [condT], lambda e: e.activation(out=cond[:], in_=c[:], func=AF.Silu))
    wb = [cx.sb("aw", [128, 16, 512], F32) for _ in range(2)]
    ab, abT = load_small(cx, adab_ap.rearrange("l p j -> p l j"), [128, 2, 12], F32, "ab")
    mo, moT = cx.sb("mo", [128, 2, 12], F32)
    i = 0
    for l in range(2):
        for blk in range(3):
            W, WT = wb[i % 2]
            i += 1
            S.dma("sp", [], [WT], lambda e, W=W, l=l, blk=blk: e.dma_start(out=W[:], in_=adaw_ap[l, :, blk * 512:(blk + 1) * 512].rearrange("(c p) m -> p c m", p=128)))
            def mm(e, W=W, l=l, blk=blk):
                ins = None
                for m in range(4):
                    j = l * 12 + blk * 4 + m
                    for k in range(16):
                        ins = e.matmul(cx.ps[0][:, j:j + 1], W[:, k, m * 128:(m + 1) * 128], cond[:, k:k + 1], start=(k == 0), stop=(k == 15))
                return ins
            S.op("pe", [WT, condT], [cx.psT[0]], mm)
    S.op("dve", [cx.psT[0], abT], [moT], lambda e: e.tensor_tensor(mo[:].rearrange("p l j -> p (l j)"), cx.ps[0][:, 0:24], ab[:].rearrange("p l j -> p (l j)"), op=ALU.add))
    S.dma("sp", [moT], [], lambda e: e.dma_start(out=mod_o.rearrange("l p j -> p l j"), in_=mo[:]))
```
